# Optimizing a Trainium2 kernel written in Bass

```python
import jax, jax.numpy as jnp
from jax import lax
import numpy as np

D_MODEL = 1024
BATCH = 16
SEQ = 4096
DEPTH = 1

CTX_LEN = 256
GRID_W = 64
RET_W = D_MODEL // 2
RET_HEADS = 4
RET_HEAD_DIM = RET_W // RET_HEADS
FOURIER_W = D_MODEL - RET_W
FOURIER_GROUPS = 4
FOURIER_GROUP_DIM = FOURIER_W // FOURIER_GROUPS
MIX_W = RET_W + FOURIER_W
IN_W = 4 * RET_W + FOURIER_W
D_FF = ((8 * D_MODEL // 3 + 127) // 128) * 128
CONV_WIDTH = 3
CHUNK = 128
ROPE_BASE = 10000.0
NORM_EPS = 1e-6
N_MOD = 6

kernel_name = "hybrid_retention_fourier_convffn_dit"


def rms_norm(x, g):
    xf = x.astype(jnp.float32)
    xf = xf * lax.rsqrt(jnp.mean(xf * xf, axis=-1, keepdims=True) + NORM_EPS)
    return xf.astype(x.dtype) * g


def modulate(h, shift, scale):
    return h * (1 + scale) + shift


def adaln(cond, w, b):
    return jnp.split(jax.nn.silu(cond) @ w + b, N_MOD, axis=-1)


def split_heads(t):
    bsz, length, _ = t.shape
    return t.reshape(bsz, length, RET_HEADS, RET_HEAD_DIM).transpose(0, 2, 1, 3).astype(jnp.float32)


def axial_rope(length):
    t = jnp.arange(length)
    row = (t // GRID_W).astype(jnp.float32)
    col = (t % GRID_W).astype(jnp.float32)
    n_freq = RET_HEAD_DIM // 4
    freqs = ROPE_BASE ** (-jnp.arange(n_freq, dtype=jnp.float32) / n_freq)
    ang = jnp.concatenate([row[:, None] * freqs, col[:, None] * freqs], axis=-1)
    return jnp.cos(ang), jnp.sin(ang)


def apply_rope(t, cos, sin):
    half = RET_HEAD_DIM // 2
    t1, t2 = t[..., :half], t[..., half:]
    return jnp.concatenate([t1 * cos - t2 * sin, t1 * sin + t2 * cos], axis=-1)


def retention_scan(q, k, v, log_gamma, init_state, strict):
    bsz, heads, length, dk = q.shape
    dv = v.shape[-1]
    n_chunks = length // CHUNK
    pos = jnp.arange(CHUNK, dtype=jnp.float32)
    diff = pos[:, None] - pos[None, :]
    mask = diff > 0 if strict else diff >= 0
    dmask = jnp.where(mask[None], jnp.exp(jnp.where(mask, diff, 0.0)[None] * log_gamma[:, None, None]), 0.0)
    qc = q.reshape(bsz, heads, n_chunks, CHUNK, dk)
    kc = k.reshape(bsz, heads, n_chunks, CHUNK, dk)
    vc = v.reshape(bsz, heads, n_chunks, CHUNK, dv)
    scores = jnp.einsum('bhncd,bhnmd->bhncm', qc, kc) * dmask[:, None]
    intra = jnp.einsum('bhncm,bhnme->bhnce', scores, vc)
    zeta = jnp.exp((CHUNK - 1 - pos)[None, :] * log_gamma[:, None])
    xi = jnp.exp((pos + 1)[None, :] * log_gamma[:, None])
    kv = jnp.einsum('bhnmd,bhnme->nbhde', kc * zeta[None, :, None, :, None], vc)
    chunk_decay = jnp.exp(CHUNK * log_gamma)[None, :, None, None]

    def step(state, kv_i):
        return state * chunk_decay + kv_i, state

    _, prev = lax.scan(step, init_state, kv)
    cross = jnp.einsum('bhncd,nbhde->bhnce', qc * xi[None, :, None, :, None], prev)
    return (intra + cross).reshape(bsz, heads, length, dv)


def context_states(h_ctx, w_in, lg_f, lg_b):
    kv = h_ctx @ w_in[:, RET_W:3 * RET_W]
    k, v = jnp.split(kv, 2, axis=-1)
    k = split_heads(k) * RET_HEAD_DIM ** -0.5
    v = split_heads(v)
    length = k.shape[2]
    m = jnp.arange(length, dtype=jnp.float32)
    w_f = jnp.exp((length - 1 - m)[None, :] * lg_f[:, None])
    w_b = jnp.exp(m[None, :] * lg_b[:, None])
    s_f = jnp.einsum('bhld,hl,bhle->bhde', k, w_f, v)
    s_b = jnp.einsum('bhld,hl,bhle->bhde', k, w_b, v)
    return s_f, s_b


def fourier_mix(f):
    bsz, length, _ = f.shape
    fg = f.reshape(bsz, length, FOURIER_GROUPS, FOURIER_GROUP_DIM).astype(jnp.float32)
    out = jnp.fft.fft2(fg, axes=(1, 3), norm="ortho").real
    return out.reshape(bsz, length, FOURIER_W).astype(f.dtype)


def token_mixer(h, w_in, w_out, lg_f, lg_b, init_f, init_b, rope):
    proj = h @ w_in
    q, k, v, g, f = jnp.split(proj, [RET_W, 2 * RET_W, 3 * RET_W, 4 * RET_W], axis=-1)
    q = split_heads(q)
    k = split_heads(k) * RET_HEAD_DIM ** -0.5
    v = split_heads(v)
    if rope is not None:
        q = apply_rope(q, *rope)
        k = apply_rope(k, *rope)
    o_f = retention_scan(q, k, v, lg_f, init_f, strict=False)
    flip = lambda t: jnp.flip(t, axis=2)
    o_b = flip(retention_scan(flip(q), flip(k), flip(v), lg_b, init_b, strict=True))
    o = o_f + o_b
    o = o * lax.rsqrt(jnp.mean(o * o, axis=-1, keepdims=True) + NORM_EPS)
    bsz, length = h.shape[0], h.shape[1]
    o = o.transpose(0, 2, 1, 3).reshape(bsz, length, RET_W).astype(h.dtype)
    ret = o * jax.nn.silu(g)
    return jnp.concatenate([ret, fourier_mix(f)], axis=-1) @ w_out


def conv_ffn(h, w_up, conv_w, conv_b, w_down, rows):
    bsz, length, _ = h.shape
    u = (h @ w_up).reshape(bsz, rows, length // rows, 2 * D_FF)
    up = jnp.pad(u, ((0, 0), (0, 0), (1, 1), (0, 0)))
    u = up[:, :, :-2] * conv_w[0] + up[:, :, 1:-1] * conv_w[1] + up[:, :, 2:] * conv_w[2] + conv_b
    a, b = jnp.split(u.reshape(bsz, length, 2 * D_FF), 2, axis=-1)
    return (jax.nn.silu(a) * b) @ w_down


def setup_inputs(seed: int = 0) -> dict:
    key = jax.random.key(seed)
    ks = jax.random.split(key, 20)
    nrm = lambda k, shape, s: jax.random.normal(k, shape, jnp.float32) * s
    base = 1.0 - 2.0 ** (-5.0 - jnp.arange(RET_HEADS, dtype=jnp.float32))
    logit = jnp.log(base / (1.0 - base))
    return {
        "x": nrm(ks[0], (BATCH, SEQ, D_MODEL), 1.0),
        "c": nrm(ks[1], (BATCH, D_MODEL), 1.0),
        "ctx": nrm(ks[2], (BATCH, CTX_LEN, D_MODEL), 1.0),
        "c_ctx": nrm(ks[3], (D_MODEL,), 1.0),
        "w_ada": nrm(ks[4], (DEPTH, D_MODEL, N_MOD * D_MODEL), D_MODEL ** -0.5),
        "b_ada": nrm(ks[5], (DEPTH, N_MOD * D_MODEL), 0.02),
        "g_mix_pre": 1.0 + nrm(ks[6], (DEPTH, D_MODEL), 0.05),
        "g_mix_post": 1.0 + nrm(ks[7], (DEPTH, D_MODEL), 0.05),
        "g_ffn_pre": 1.0 + nrm(ks[8], (DEPTH, D_MODEL), 0.05),
        "g_ffn_post": 1.0 + nrm(ks[9], (DEPTH, D_MODEL), 0.05),
        "w_in": nrm(ks[10], (DEPTH, D_MODEL, IN_W), D_MODEL ** -0.5),
        "ret_decay_fwd": logit[None] + nrm(ks[11], (DEPTH, RET_HEADS), 0.1),
        "ret_decay_bwd": logit[None] + nrm(ks[12], (DEPTH, RET_HEADS), 0.1),
        "w_out": nrm(ks[13], (DEPTH, MIX_W, D_MODEL), MIX_W ** -0.5),
        "w_up": nrm(ks[14], (DEPTH, D_MODEL, 2 * D_FF), D_MODEL ** -0.5),
        "conv_w": nrm(ks[15], (DEPTH, CONV_WIDTH, 2 * D_FF), CONV_WIDTH ** -0.5),
        "conv_b": nrm(ks[16], (DEPTH, 2 * D_FF), 0.02),
        "w_down": nrm(ks[17], (DEPTH, D_FF, D_MODEL), D_FF ** -0.5),
    }


def reference(x, c, ctx, c_ctx, w_ada, b_ada, g_mix_pre, g_mix_post, g_ffn_pre, g_ffn_post,
              w_in, ret_decay_fwd, ret_decay_bwd, w_out, w_up, conv_w, conv_b, w_down):
    seq_len = x.shape[1]
    rows = seq_len // GRID_W
    rope = axial_rope(seq_len)
    ctx_s = ctx
    for layer in range(DEPTH):
        lg_f = jax.nn.log_sigmoid(ret_decay_fwd[layer].astype(jnp.float32))
        lg_b = jax.nn.log_sigmoid(ret_decay_bwd[layer].astype(jnp.float32))
        sh_a, sc_a, gt_a, sh_f, sc_f, gt_f = [m[:, None, :] for m in adaln(c, w_ada[layer], b_ada[layer])]
        csh_a, csc_a, cgt_a, csh_f, csc_f, cgt_f = adaln(c_ctx, w_ada[layer], b_ada[layer])
        h_ctx = modulate(rms_norm(ctx_s, g_mix_pre[layer]), csh_a, csc_a)
        s_f, s_b = context_states(h_ctx, w_in[layer], lg_f, lg_b)
        h_lat = modulate(rms_norm(x, g_mix_pre[layer]), sh_a, sc_a)
        mix = token_mixer(h_lat, w_in[layer], w_out[layer], lg_f, lg_b, s_f, s_b, rope)
        x = x + gt_a * rms_norm(mix, g_mix_post[layer])
        h_ffn = modulate(rms_norm(x, g_ffn_pre[layer]), sh_f, sc_f)
        ffn = conv_ffn(h_ffn, w_up[layer], conv_w[layer], conv_b[layer], w_down[layer], rows)
        x = x + gt_f * rms_norm(ffn, g_ffn_post[layer])
        if layer + 1 < DEPTH:
            zero = jnp.zeros_like(s_f)
            mix_c = token_mixer(h_ctx, w_in[layer], w_out[layer], lg_f, lg_b, zero, zero, None)
            ctx_s = ctx_s + cgt_a * rms_norm(mix_c, g_mix_post[layer])
            h_cffn = modulate(rms_norm(ctx_s, g_ffn_pre[layer]), csh_f, csc_f)
            ffn_c = conv_ffn(h_cffn, w_up[layer], conv_w[layer], conv_b[layer], w_down[layer], 1)
            ctx_s = ctx_s + cgt_f * rms_norm(ffn_c, g_ffn_post[layer])
    return x
```

```python
import math
from contextlib import ExitStack

import numpy as np
import ml_dtypes
import concourse.bass as bass
import concourse.mybir as mybir
from concourse.bass_utils import run_bass_kernel_spmd

F32 = mybir.dt.float32
BF16 = mybir.dt.bfloat16
U8 = mybir.dt.uint8
AF = mybir.ActivationFunctionType
ALU = mybir.AluOpType

NCORES = 8
D = 1024
L = 4096
NB = 2
CTX = 256
DFF = 2816
INW = 2560
EPS = 1e-6
ENGS = ("pe", "act", "dve", "pool", "sp")


class Op:
    __slots__ = ("eng", "fn", "deps", "needed", "is_dma", "slot", "sem", "val")


class Prog:
    def __init__(self, nc):
        self.nc = nc
        self.ops = {e: [] for e in ENGS}
        self.res = {}
        self.bar = {e: [] for e in ENGS}
        self.slot_last = {}

    def begin_capture(self):
        self.cap = []

    def end_capture(self):
        c, self.cap = self.cap, None
        return c

    def replay_merged(self, la, lb):
        i = j = 0
        while i < len(la) or j < len(lb):
            if j >= len(lb) or (i < len(la) and i * len(lb) <= j * len(la)):
                self.add(*la[i])
                i += 1
            else:
                self.add(*lb[j])
                j += 1

    def add(self, eng, fn, reads=(), writes=(), slot=None):
        import os
        if getattr(self, "cap", None) is not None:
            self.cap.append((eng, fn, tuple(reads), tuple(writes), slot))
            return None
        self.nadd = getattr(self, "nadd", 0) + 1
        cut = int(os.environ.get("KCUT", "0"))
        if cut and self.nadd > cut and fn is not None and not os.environ.get("KCUT_OFF"):
            return None
        op = Op()
        op.eng, op.fn, op.needed, op.slot = eng, fn, False, slot
        op.is_dma = slot is not None
        op.sem = op.val = None
        hard, war = set(), set()
        for r in reads:
            st = self.res.get(r)
            if st is not None and st[0] is not None:
                hard.add(st[0])
        for w in writes:
            st = self.res.get(w)
            if st is not None:
                if st[0] is not None:
                    hard.add(st[0])
                for rd in st[1]:
                    war.add(rd)
        deps = set(self.bar[eng])
        self.bar[eng] = []
        for d in hard:
            if d.eng == eng and not d.is_dma and not op.is_dma and eng == "pe":
                continue
            deps.add(d)
        for d in war:
            if d.eng == eng and not d.is_dma and not op.is_dma and eng == "pe":
                continue
            deps.add(d)
        deps.discard(op)
        for d in deps:
            d.needed = True
        op.deps = list(deps)
        for r in reads:
            st = self.res.setdefault(r, [None, []])
            st[1].append(op)
        for w in writes:
            self.res[w] = [op, []]
        self.ops[eng].append(op)
        if op.is_dma:
            self.slot_last[slot] = op
        return op

    def barrier(self):
        lasts = [self.ops[e][-1] for e in ENGS if self.ops[e]]
        lasts += list(self.slot_last.values())
        for e in ENGS:
            self.bar[e] = list(lasts)
        self.res = {}

    def emit(self):
        nc = self.nc
        self.barrier()
        self.add("sp", None)
        es = ExitStack()
        sems = {e: es.enter_context(nc.semaphore("s_" + e)) for e in ENGS}
        slot_sem = {}
        for e in ENGS:
            n = 0
            for op in self.ops[e]:
                if op.is_dma:
                    if op.slot not in slot_sem:
                        slot_sem[op.slot] = [es.enter_context(nc.semaphore("d_" + str(len(slot_sem)))), 0]
                    ss = slot_sem[op.slot]
                    ss[1] += 16
                    op.sem, op.val = ss[0], ss[1]
                elif op.needed:
                    n += 1
                    op.sem, op.val = sems[e], n

        def run(e, eng):
            seen = {}
            for op in self.ops[e]:
                want = {}
                for d in op.deps:
                    assert d.sem is not None
                    k = d.sem.name
                    if want.get(k, (None, 0))[1] < d.val:
                        want[k] = (d.sem, d.val)
                for k, (s, v) in want.items():
                    if seen.get(k, 0) < v:
                        eng.wait_ge(s, v)
                        seen[k] = v
                if op.fn is not None:
                    ins = op.fn(eng)
                    if op.is_dma:
                        ins.then_inc(op.sem, 16)
                    elif op.needed:
                        ins.then_inc(op.sem, 1)

        with nc.Block() as block:
            @block.sync
            def _(eng):
                run("sp", eng)

            @block.tensor
            def _(eng):
                run("pe", eng)

            @block.scalar
            def _(eng):
                run("act", eng)

            @block.vector
            def _(eng):
                run("dve", eng)

            @block.gpsimd
            def _(eng):
                run("pool", eng)
        es.close()


class Arena:
    def __init__(self, base_u8, size):
        self.base, self.size, self.off = base_u8, size, 0

    def alloc(self, free_shape, dtype, align=64):
        isz = 2 if dtype == BF16 else 4
        n = int(np.prod(free_shape)) * isz
        off = (self.off + align - 1) // align * align
        assert off + n <= self.size, ("SBUF arena overflow", off + n, self.size)
        self.off = off + n
        v = self.base[:, off:off + n].bitcast(dtype)
        if len(free_shape) > 1:
            names = " ".join("a%d" % i for i in range(len(free_shape)))
            kw = {"a%d" % i: int(s) for i, s in enumerate(free_shape)}
            v = v.rearrange("p (%s) -> p %s" % (names, names), **kw)
        return v

    def mark(self):
        return self.off

    def release(self, m):
        self.off = m


def _bf(a):
    return np.ascontiguousarray(a.astype(np.float32)).astype(ml_dtypes.bfloat16)


def host_consts():
    c = {}
    c["ident"] = _bf(np.eye(128))
    t = np.arange(L)
    row = (t // 64).astype(np.float64)
    col = (t % 64).astype(np.float64)
    freqs = 10000.0 ** (-np.arange(32, dtype=np.float64) / 32)
    ang = np.concatenate([row[:, None] * freqs, col[:, None] * freqs], axis=-1)
    sc_ = 128.0 ** -0.5
    rt = np.stack([np.cos(ang), np.sin(ang)], axis=0)
    c["rope"] = _bf(rt.reshape(2, 32, 128, 64).transpose(2, 0, 1, 3))
    tw = np.zeros((32, 128, 256), np.float64)
    k1 = np.arange(64)[None, :].astype(np.float64)
    t1 = np.arange(64)[:, None].astype(np.float64)
    for b_ in range(32):
        for aa in range(2):
            ph = 2 * np.pi * k1 * (64 * t1 + 2 * b_ + aa) / 4096.0
            tw[b_, aa * 64:(aa + 1) * 64, aa * 128:aa * 128 + 64] = np.cos(ph)
            tw[b_, aa * 64:(aa + 1) * 64, aa * 128 + 64:aa * 128 + 128] = -np.sin(ph)
    c["tw"] = _bf(tw)
    t2 = np.arange(64)[:, None].astype(np.float64)
    k2 = np.arange(64)[None, :].astype(np.float64)
    th = 2 * np.pi * t2 * k2 / 64.0
    c["fb"] = _bf(np.concatenate([np.concatenate([np.cos(th), -np.sin(th)], axis=1),
                                  np.concatenate([np.sin(th), np.cos(th)], axis=1)], axis=0))
    m = np.arange(128)[:, None].astype(np.float64)
    cc = np.arange(128)[None, :].astype(np.float64)
    dk = np.zeros((128, 4, 128), np.float32)
    dk[:, 0] = np.maximum(cc - m, 0)
    dk[:, 1] = np.maximum(m - cc, 0)
    dk[:, 2] = (cc >= m)
    dk[:, 3] = (cc < m)
    c["dtab"] = dk
    pm = np.arange(128, dtype=np.float32)
    c["pcol"] = np.stack([127 - pm, pm, pm + 1, 128 - pm, 255 - pm, 127 - pm, pm, 128 + pm], axis=1).astype(np.float32)
    j = np.arange(128)[None, :].astype(np.float64)
    ch = np.arange(128)[:, None].astype(np.float64)
    a = 2 * np.pi * ch * j / 128
    c["cs"] = _bf(np.concatenate([np.cos(a), np.sin(a)], axis=1))
    return c


def build(stage="full", dbg=None):
    nc = bass.Bass("TRN2", target_bir_lowering=False)
    P = Prog(nc)

    def din(name, shape, dt=F32):
        return nc.dram_tensor(name, list(shape), dt, kind="ExternalInput").ap()

    x_d = din("x", [NB, L, D])
    ctx_d = din("ctx", [NB, CTX, D])
    cT_d = din("cT", [128, 8, 3])
    wada_d = din("w_ada", [D, 6 * D])
    bada_r_d = din("b_ada_row", [1, 6 * D])
    badaT_d = din("b_adaT", [128, 48])
    gT_d = din("gT", [128, 4, 8])
    grow_d = din("g_row", [4, D])
    win_d = din("w_in", [D, INW])
    dec_d = din("decay", [1, 8])
    wout_d = din("w_out", [D, D])
    wup_d = din("w_up", [D, 2 * DFF])
    cw_d = din("cwT", [128, 44, 4])
    wdn_d = din("w_down", [DFF, D])
    ident_d = din("ident", [128, 128], BF16)
    rope_d = din("rope", [128, 2, 32, 64], BF16)
    tw_d = din("tw", [32, 128, 256], BF16)
    fb_d = din("fb", [128, 128], BF16)
    dtab_d = din("dtab", [128, 4, 128])
    pcol_d = din("pcol", [128, 8])
    cs_d = din("cs", [128, 256], BF16)
    out_d = nc.dram_tensor("out", [NB, L, D], F32, kind="ExternalOutput").ap()
    x1_d = nc.dram_tensor("x1s", [NB, L, D], F32).ap()
    f_d = nc.dram_tensor("fscr", [NB, L, 512], BF16).ap()
    kv_d = nc.dram_tensor("kvscr", [NB, L, 1024], BF16).ap()
    dbg_d = None
    if dbg is not None:
        dbg_d = nc.dram_tensor("dbg", list(dbg), F32, kind="ExternalOutput").ap()

    es = ExitStack()
    ARENA_BYTES = 212800
    arena_t = es.enter_context(nc.sbuf_tensor("arena", [128, ARENA_BYTES], U8))
    A = Arena(arena_t, ARENA_BYTES)
    ps = es.enter_context(nc.psum_tensor("ps", [128, 4096], F32))

    def bank(i, n=1):
        return ps[:, i * 512:(i + n) * 512]

    def bank_bf(i, n=1):
        return ps[:, i * 512:(i + n) * 512].bitcast(BF16)

    ident = A.alloc([128], BF16)
    modT = A.alloc([4, 8, 3], F32)
    gsA = A.alloc([8, 3], F32)
    shA = A.alloc([8, 3], F32)
    gsF = A.alloc([8, 3], F32)
    shF = A.alloc([8, 3], F32)
    ggtA = A.alloc([NB, D], F32)
    ggtF = A.alloc([NB, D], F32)
    cwT = A.alloc([44, 4], F32)
    gTt = A.alloc([4, 8], F32)
    badaT = A.alloc([48], F32)
    scT = A.alloc([8, 3], BF16)
    cT = A.alloc([8, 3], F32)
    epsc = A.alloc([4], F32)

    ld = [0]

    def dma(eng, out, in_, reads=(), writes=(), slot=None):
        if slot is None:
            ld[0] += 1
            slot = "once%d" % ld[0]
        return P.add(eng, lambda e: e.dma_start(out=out, in_=in_), reads=reads, writes=writes, slot=slot)

    dma("sp", ident, ident_d, writes=["ident"])
    P.add("dve", lambda e: e.memset(epsc[:, 0:1], EPS), writes=["epsc"])
    P.add("dve", lambda e: e.memset(epsc[:, 1:2], 1.0), writes=["epsc"])
    P.add("dve", lambda e: e.memset(epsc[:, 2:3], -0.5), writes=["epsc"])
    dma("sp", cT, cT_d, writes=["cT"])
    dma("sp", cwT, cw_d, writes=["cwT"])
    dma("sp", gTt, gT_d, writes=["gTt"])
    dma("sp", badaT, badaT_d, writes=["badaT"])

    mk_ada = A.mark()
    wa = [A.alloc([8, D], BF16) for _ in range(2)]
    scR = A.alloc([8, NB, 128], BF16)
    rowt = A.alloc([D], F32)
    rowg = A.alloc([D], F32)
    P.add("act", lambda e: e.activation(out=scT, in_=cT, func=AF.Silu), reads=["cT"], writes=["scT"])
    for b in range(NB):
        P.add("dve", lambda e, b=b: e.tensor_copy(out=scR[:, :, b, :], in_=scT[:, :, b:b + 1].broadcast_to([128, 8, 128])),
              reads=["scT"], writes=["scR"])
    wada_v = wada_d.rearrange("(k p) n -> p k n", p=128)
    mods_fm = {0: 0, 1: 1, 3: 2, 4: 3}
    for m in range(6):
        w = wa[m % 2]
        wk = "wa%d" % (m % 2)
        dma("pool", w, wada_v[:, :, m * D:(m + 1) * D], writes=[wk, "wq"], slot=wk)
        if m in mods_fm:
            mi = mods_fm[m]
            def f(e, w=w):
                ins = None
                for kd in range(8):
                    for k in range(8):
                        ins = e.matmul(bank(0)[:, kd * 4:kd * 4 + 3], lhsT=w[:, k, kd * 128:(kd + 1) * 128],
                                       rhs=scT[:, k, :], start=(k == 0), stop=(k == 7))
                return ins
            P.add("pe", f, reads=[wk, "scT"], writes=["psA"])
            P.add("dve", lambda e, mi=mi, m=m: e.tensor_tensor(
                out=modT[:, mi], in0=bank(0)[:, 0:32].rearrange("p (k r) -> p k r", r=4)[:, :, 0:3],
                in1=badaT[:, m * 8:(m + 1) * 8].unsqueeze(2).broadcast_to([128, 8, 3]), op=ALU.add),
                reads=["psA", "badaT"], writes=["modT"])
        else:
            gi = 0 if m == 2 else 1
            tab = ggtA if m == 2 else ggtF
            dma("sp", rowt, bada_r_d[0:1, m * D:(m + 1) * D].partition_broadcast(128)[:, 0, :], writes=["rowt"], slot="rowt")
            dma("sp", rowg, grow_d[(1 if m == 2 else 3):(2 if m == 2 else 4), :].partition_broadcast(128)[:, 0, :],
                writes=["rowg"], slot="rowg")
            for b in range(NB):
                def f(e, w=w, b=b):
                    ins = None
                    for h in range(2):
                        for k in range(8):
                            ins = e.matmul(bank(1 + h), lhsT=scR[:, k, b, :], rhs=w[:, k, h * 512:(h + 1) * 512],
                                           start=(k == 0), stop=(k == 7))
                    return ins
                P.add("pe", f, reads=[wk, "scR"], writes=["psG"])
                P.add("dve", lambda e, tab=tab, b=b: e.tensor_tensor(out=tab[:, b, :], in0=bank(1, 2), in1=rowt, op=ALU.add),
                      reads=["psG", "rowt"], writes=["ggt"])
                P.add("dve", lambda e, tab=tab, b=b: e.tensor_tensor(out=tab[:, b, :], in0=tab[:, b, :], in1=rowg, op=ALU.mult),
                      reads=["ggt", "rowg"], writes=["ggt"])
    for (gs, sh, gi, m_sh, m_sc) in ((gsA, shA, 0, 0, 1), (gsF, shF, 2, 2, 3)):
        P.add("dve", lambda e, gs=gs, gi=gi, m_sc=m_sc: e.scalar_tensor_tensor(
            out=gs, in0=modT[:, m_sc], scalar=1.0, in1=gTt[:, gi, :].unsqueeze(2).broadcast_to([128, 8, 3]),
            op0=ALU.add, op1=ALU.mult), reads=["modT", "gTt"], writes=["gs"])
        P.add("dve", lambda e, sh=sh, m_sh=m_sh: e.tensor_copy(out=sh, in_=modT[:, m_sh]), reads=["modT"], writes=["sh"])
    P.barrier()
    A.release(mk_ada)

    if stage == "ada":
        t = A.alloc([24 + 24 + 2 * D], F32)
        P.add("dve", lambda e: e.tensor_copy(out=t[:, 0:24], in_=gsF.rearrange("p k r -> p (k r)")))
        P.add("dve", lambda e: e.tensor_copy(out=t[:, 24:48], in_=shF.rearrange("p k r -> p (k r)")))
        P.add("dve", lambda e: e.tensor_copy(out=t[:, 48:], in_=ggtF.rearrange("p b d -> p (b d)")), writes=["t"])
        dma("sp", dbg_d, t, reads=["t"])
        P.emit()
        es.close()
        return nc

    def mixer_phase():
        mk = A.mark()
        win = A.alloc([8, INW], BF16)
        wout = A.alloc([8, D], BF16)
        win_v = win_d.rearrange("(k p) n -> p k n", p=128)
        for k in range(8):
            dma("pool", win[:, k, :].rearrange("p (a c) -> p a c", c=1280),
                win_v[:, k, :].rearrange("p (a c) -> p a c", c=1280), writes=["win", "wq"])
        dma("pool", wout, wout_d.rearrange("(k p) n -> p k n", p=128), writes=["wout", "wq"])
        rc = A.alloc([2, 32, 64], BF16)
        dma("sp", rc, rope_d, writes=["rc"])
        cs = A.alloc([256], BF16)
        dma("sp", cs, cs_d, writes=["cs"])
        fbm = A.alloc([128], BF16)
        dma("sp", fbm, fb_d, writes=["fbm"])
        dtab = A.alloc([4, 128], F32)
        dma("sp", dtab, dtab_d, writes=["dtab"])
        pcol = A.alloc([8], F32)
        dma("sp", pcol, pcol_d, writes=["pcol"])
        dec = A.alloc([8], F32)
        dma("sp", dec, dec_d[0:1, :].partition_broadcast(128)[:, 0, :], writes=["dec"])
        lg = A.alloc([8], F32)
        ptab = A.alloc([8, 4], F32)
        g128 = A.alloc([8], F32)
        dmask = A.alloc([4, 128], F32)
        SQK = float(128 ** -0.5)
        tmpd = A.alloc([2, 128], F32)
        P.barrier()
        P.add("act", lambda e: e.activation(out=lg, in_=dec, func=AF.Exp, scale=-1.0), writes=["lg"])
        P.add("act", lambda e: e.activation(out=lg, in_=lg, func=AF.Ln, bias=epsc[:, 1:2]), reads=["lg"], writes=["lg"])
        P.add("dve", lambda e: e.tensor_scalar(out=lg, in0=lg, scalar1=-1.0, scalar2=None, op0=ALU.mult), reads=["lg"], writes=["lg"])
        isf = [True, False, True, False, True, True, False, False]
        for j in range(8):
            lo = 0 if isf[j] else 4
            P.add("dve", lambda e, j=j, lo=lo: e.tensor_scalar(out=ptab[:, j, :], in0=lg[:, lo:lo + 4], scalar1=pcol[:, j:j + 1],
                                                                scalar2=None, op0=ALU.mult), reads=["lg"], writes=["ptab"])
        P.add("act", lambda e: e.activation(out=ptab, in_=ptab, func=AF.Exp), reads=["ptab"], writes=["ptab"])
        P.add("act", lambda e: e.activation(out=g128, in_=lg, func=AF.Exp, scale=128.0), reads=["lg"], writes=["g128"])
        for h in range(4):
            P.add("act", lambda e, h=h: e.activation(out=tmpd[:, 0, :], in_=dtab[:, 0, :], func=AF.Exp, scale=lg[:, h:h + 1]),
                  reads=["lg", "dmask"], writes=["t0"])
            P.add("act", lambda e, h=h: e.activation(out=tmpd[:, 1, :], in_=dtab[:, 1, :], func=AF.Exp, scale=lg[:, 4 + h:5 + h]),
                  reads=["lg", "dmask"], writes=["t1"])
            P.add("dve", lambda e: e.scalar_tensor_tensor(out=tmpd[:, 0, :], in0=tmpd[:, 0, :], scalar=SQK, in1=dtab[:, 2, :],
                                                          op0=ALU.mult, op1=ALU.mult), reads=["t0"], writes=["t0"])
            P.add("dve", lambda e: e.scalar_tensor_tensor(out=tmpd[:, 1, :], in0=tmpd[:, 1, :], scalar=SQK, in1=dtab[:, 3, :],
                                                          op0=ALU.mult, op1=ALU.mult), reads=["t1"], writes=["t1"])
            P.add("dve", lambda e, h=h: e.tensor_tensor(out=dmask[:, h, :], in0=tmpd[:, 0, :], in1=tmpd[:, 1, :], op=ALU.add),
                  reads=["t0", "t1"], writes=["dmask"])
        P.add("dve", lambda e: e.tensor_scalar(out=ptab[:, 2:4, :], in0=ptab[:, 2:4, :], scalar1=SQK, scalar2=None, op0=ALU.mult),
              reads=["ptab"], writes=["ptab"])
        P.barrier()

        cat_sb = A.alloc([4, L], BF16)
        KVB = A.alloc([32, 512], BF16)
        Sf32 = A.alloc([512], F32)
        Sfb = A.alloc([512], BF16)
        Bst = A.alloc([512], F32)
        Bst_b = A.alloc([512], F32)
        B31 = A.alloc([512], BF16)
        mkw = A.mark()

        xtm = [A.alloc([D], F32) for _ in range(3)]
        xnm = A.alloc([D], BF16)
        junk = A.alloc([D], BF16)
        sq = [A.alloc([2], F32) for _ in range(3)]
        hTm = [A.alloc([8, 128], BF16) for _ in range(2)]
        ra = A.alloc([4, 64], F32)
        rb = A.alloc([4, 64], F32)
        q_r2 = [A.alloc([4, 128], BF16) for _ in range(2)]
        q_r = q_r2[0]
        q_f = A.alloc([4, 128], BF16)
        q_b = A.alloc([4, 128], BF16)
        k_r2 = [A.alloc([4, 128], BF16) for _ in range(2)]
        k_r = k_r2[0]
        k_z = A.alloc([4, 128], BF16)
        k_z2 = A.alloc([4, 128], BF16)
        v_b2 = [A.alloc([4, 128], BF16) for _ in range(2)]
        v_b = v_b2[0]
        sg2 = [A.alloc([512], BF16) for _ in range(2)]
        qkT = A.alloc([16, 128], BF16)
        PT = A.alloc([4, 128], BF16)
        ret = A.alloc([4, 128], BF16)
        catr = A.alloc([4, 128], BF16)
        fst = [A.alloc([512], BF16) for _ in range(2)]
        hs = A.alloc([8], F32)
        ytm = A.alloc([D], F32)
        tpv = bank_bf(0)[:, 0:1024].rearrange("p (k t) -> p k t", k=8)
        cnt = [0]

        def rstd_ops(src, dst, scale, rd, wr):
            P.add("dve", lambda e: e.tensor_scalar(out=dst, in0=src, scalar1=scale, scalar2=EPS, op0=ALU.mult, op1=ALU.add),
                  reads=rd, writes=wr)
            n_ = dst.shape[1]
            P.add("pool", lambda e: e.tensor_tensor(out=dst, in0=dst, in1=epsc[:, 2:3].broadcast_to([128, n_]), op=ALU.pow),
                  reads=wr, writes=wr)

        def load_norm_T(src_ap, gs, sh, r):
            i = cnt[0]
            cnt[0] += 1
            par = i % 2
            xi = i % 3
            X, S, H = "mx%d" % xi, "msq%d" % xi, "mh%d" % par
            dma("sp", xtm[xi], src_ap, writes=[X], slot=X)
            P.add("act", lambda e: e.activation(out=junk, in_=xtm[xi], func=AF.Square, accum_out=sq[xi][:, 0:1]),
                  reads=[X], writes=[S, "junk"])
            rstd_ops(sq[xi][:, 0:1], sq[xi][:, 1:2], 1.0 / D, [S], [S])
            P.add("dve", lambda e: e.tensor_scalar(out=xnm, in0=xtm[xi], scalar1=sq[xi][:, 1:2], scalar2=None, op0=ALU.mult),
                  reads=[X, S], writes=["xnm"])

            def ftp(e):
                ins = None
                for k in range(8):
                    ins = e.transpose(tpv[:, k, :], xnm[:, k * 128:(k + 1) * 128], ident)
                return ins
            P.add("pe", ftp, reads=["xnm"], writes=["b0"])
            for k in range(8):
                P.add("dve", lambda e, k=k: e.tensor_scalar(out=hTm[par][:, k, :], in0=tpv[:, k, :], scalar1=gs[:, k, r:r + 1],
                                                            scalar2=sh[:, k, r:r + 1], op0=ALU.mult, op1=ALU.add),
                      reads=["b0"], writes=[H])
            return par, xi

        def proj(par, col0, bi):
            def f(e):
                ins = None
                for k in range(8):
                    ins = e.matmul(bank(bi), lhsT=hTm[par][:, k, :], rhs=win[:, k, col0:col0 + 512], start=(k == 0), stop=(k == 7))
                return ins
            P.add("pe", f, reads=["mh%d" % par], writes=["b%d" % bi])

        def rope(bi, n, out, tb, RO="ro"):
            src = bank(bi).rearrange("p (h d) -> p h d", h=4)
            t1, t2 = src[:, :, 0:64], src[:, :, 64:128]
            cos = rc[:, tb, n, :].unsqueeze(1).broadcast_to([128, 4, 64])
            sin = rc[:, tb + 1, n, :].unsqueeze(1).broadcast_to([128, 4, 64])
            B_ = "b%d" % bi
            P.add("dve", lambda e: e.tensor_tensor(out=ra, in0=t1, in1=cos, op=ALU.mult), reads=[B_], writes=["ra"])
            P.add("dve", lambda e: e.tensor_tensor(out=rb, in0=t2, in1=sin, op=ALU.mult), reads=[B_], writes=["rb"])
            P.add("dve", lambda e: e.tensor_tensor(out=out[:, :, 0:64], in0=ra, in1=rb, op=ALU.subtract), reads=["ra", "rb"], writes=[RO])
            P.add("dve", lambda e: e.tensor_tensor(out=ra, in0=t1, in1=sin, op=ALU.mult), reads=[B_, RO], writes=["ra"])
            P.add("dve", lambda e: e.tensor_tensor(out=rb, in0=t2, in1=cos, op=ALU.mult), reads=[B_, RO], writes=["rb"])
            P.add("dve", lambda e: e.tensor_tensor(out=out[:, :, 64:128], in0=ra, in1=rb, op=ALU.add), reads=["ra", "rb"], writes=[RO])

        def hscale(out, in_, j, rd, wr, eng="pool"):
            P.add(eng, lambda e: e.tensor_tensor(out=out, in0=in_, in1=ptab[:, j, :].unsqueeze(2).broadcast_to([128, 4, 128]),
                                                 op=ALU.mult), reads=rd, writes=wr)

        def kv_mm(bi, kz, start, stop, rd, vv=None):
            vv = v_b if vv is None else vv
            def f(e):
                ins = None
                for h in range(4):
                    ins = e.matmul(bank(bi)[:, h * 128:(h + 1) * 128], lhsT=kz[:, h, :], rhs=vv[:, h, :], start=start, stop=stop)
                return ins
            P.add("pe", f, reads=rd, writes=["b%d" % bi])

        for b in range(1 if stage == "mix1" else NB):
            def ctx_tile(b, t):
                par, _xi = load_norm_T(ctx_d[b, t * 128:(t + 1) * 128, :], gsA, shA, 2)
                proj(par, 512, 1)
                proj(par, 1024, 2)
                P.add("act", lambda e: e.activation(out=k_r.rearrange("p h d -> p (h d)"), in_=bank(1), func=AF.Copy),
                      reads=["b1"], writes=["k_r"])
                P.add("act", lambda e: e.activation(out=v_b.rearrange("p h d -> p (h d)"), in_=bank(2), func=AF.Copy),
                      reads=["b2"], writes=["v_b"])
                hscale(k_z, k_r, 4 + t, ["k_r"], ["k_z"])
                hscale(k_z2, k_r, 6 + t, ["k_r"], ["k_z2"])
                kv_mm(6, k_z, True, True, ["k_z", "v_b"])
                kv_mm(7, k_z2, True, True, ["k_z2", "v_b"])
                if t == 0:
                    P.add("dve", lambda e: e.tensor_copy(out=Sf32, in_=bank(6)), reads=["b6"], writes=["Sf32"])
                    P.add("dve", lambda e: e.tensor_copy(out=Bst, in_=bank(7)), reads=["b7"], writes=["Bst"])
                else:
                    P.add("dve", lambda e: e.tensor_tensor(out=Sf32, in0=bank(6), in1=Sf32, op=ALU.add), reads=["b6", "Sf32"], writes=["Sf32"])
                    P.add("dve", lambda e: e.tensor_tensor(out=Bst, in0=bank(7), in1=Bst, op=ALU.add), reads=["b7", "Bst"], writes=["Bst"])
            for t in range(2):
                ctx_tile(b, t)
            P.add("act", lambda e: e.activation(out=Sfb, in_=Sf32, func=AF.Copy), reads=["Sf32"], writes=["Sfb"])
            P.add("act", lambda e: e.activation(out=B31, in_=Bst, func=AF.Copy), reads=["Bst"], writes=["B31"])

            def p1_A(b, n):
                par, _xi = load_norm_T(x_d[b, n * 128:(n + 1) * 128, :], gsA, shA, b)
                base = 1 + 3 * (n % 2)
                proj(par, 2048, base)
                proj(par, 512, base + 1)
                proj(par, 1024, base + 2)

            def p1_B(b, n):
                base = 1 + 3 * (n % 2)
                fs_ = fst[n % 2]
                FS = "fst%d" % (n % 2)
                P.add("act", lambda e: e.activation(out=fs_, in_=bank(base), func=AF.Copy), reads=["b%d" % base], writes=[FS])
                dma("pool", f_d[b, n * 128:(n + 1) * 128, :], fs_, reads=[FS], writes=["f_d"], slot="st" + FS)
                w_ = n % 2
                kr_, vb_ = k_r2[w_], v_b2[w_]
                ROK, VB = "rok%d" % w_, "v_b%d" % w_
                rope(base + 1, n, kr_, 0, ROK)
                P.add("act", lambda e: e.activation(out=vb_.rearrange("p h d -> p (h d)"), in_=bank(base + 2), func=AF.Copy),
                      reads=["b%d" % (base + 2)], writes=[VB])
                dma("pool", kv_d[b, n * 128:(n + 1) * 128, 0:512], kr_.rearrange("p h d -> p (h d)"), reads=[ROK], writes=["kv_d"],
                    slot="stk%d" % w_)
                dma("pool", kv_d[b, n * 128:(n + 1) * 128, 512:1024], vb_.rearrange("p h d -> p (h d)"), reads=[VB], writes=["kv_d"],
                    slot="stv%d" % w_)
                hscale(k_z, kr_, 1, [ROK], ["k_z"])
                kv_mm(7, k_z, True, True, ["k_z", VB], vb_)
                P.add("act", lambda e: e.activation(out=KVB[:, n, :], in_=bank(7), func=AF.Copy), reads=["b7"], writes=["KVB"])
            def merge2(la, lb):
                out, i_, j_ = [], 0, 0
                while i_ < len(la) or j_ < len(lb):
                    if j_ >= len(lb) or (i_ < len(la) and i_ * len(lb) <= j_ * len(la)):
                        out.append(la[i_]); i_ += 1
                    else:
                        out.append(lb[j_]); j_ += 1
                return out

            def cap_p1A(n):
                P.begin_capture()
                p1_A(b, n)
                ops = P.end_capture()
                return ops[:-3], ops[-3:]
            a1 = {}
            a2 = {}
            for n0 in (0, 1):
                a1[n0], a2[n0] = cap_p1A(n0)
            P.replay_merged(a1[0], [])
            P.replay_merged(a1[1], a2[0])
            for n in range(32):
                if n + 2 < 32:
                    a1[n + 2], a2[n + 2] = cap_p1A(n + 2)
                P.begin_capture()
                p1_B(b, n)
                lb = P.end_capture()
                P.replay_merged(a1.get(n + 2, []), merge2(a2.get(n + 1, []), lb))
            P.barrier()
            P.begin_capture()
            bsts = [Bst, Bst_b]
            for n in range(30, -1, -1):
                src_, dst_ = bsts[n % 2], bsts[(n + 1) % 2]
                SN, DN = "Bst%d" % (n % 2), "Bst%d" % ((n + 1) % 2)
                for h in range(4):
                    P.add("dve", lambda e, n=n, h=h, src_=src_, dst_=dst_: e.scalar_tensor_tensor(
                        out=dst_[:, h * 128:(h + 1) * 128], in0=src_[:, h * 128:(h + 1) * 128], scalar=g128[:, 4 + h:5 + h],
                        in1=KVB[:, n + 1, h * 128:(h + 1) * 128], op0=ALU.mult, op1=ALU.add),
                        reads=["KVB%d" % (n + 1), SN + "_%d" % h], writes=[DN + "_%d" % h])
                P.add("act", lambda e, n=n, dst_=dst_: e.activation(out=KVB[:, n + 1, :], in_=dst_, func=AF.Copy),
                      reads=[DN + "_%d" % h for h in range(4)], writes=["KVB%d" % (n + 1)])
            rec_ops = P.end_capture()

            mkf = A.mark()
            A.release(mkw)
            YT = A.alloc([2, 64, 2, 64], BF16)
            ftl = [A.alloc([512], BF16) for _ in range(2)]
            twl = [A.alloc([256], BF16) for _ in range(2)]
            YTT = [A.alloc([8, 128], BF16) for _ in range(2)]
            PQ = [A.alloc([8, 2, 64], BF16) for _ in range(2)]
            fview = f_d[b].rearrange("(t1 w) c -> w t1 c", w=64)
            SCL = float(1.0 / math.sqrt(L * 128.0))

            def fft_A(gh, bb):
                w_ = bb % 2
                FT, TWN = "ftl%d" % w_, "twl%d" % w_
                dma("sp", ftl[w_][0:64], fview[2 * bb], reads=["f_d"], writes=[FT], slot=FT + "a")
                dma("sp", ftl[w_][64:128], fview[2 * bb + 1], reads=["f_d"], writes=[FT + "x"], slot=FT + "b")
                dma("sp", twl[w_], tw_d[bb], writes=[TWN], slot=TWN)
                bk = 1 + w_

                def f(e):
                    ins = None
                    for gl in range(2):
                        g = 2 * gh + gl
                        ins = e.matmul(bank(bk)[:, gl * 256:(gl + 1) * 256], lhsT=ftl[w_][:, g * 128:(g + 1) * 128], rhs=twl[w_],
                                       start=True, stop=True)
                    return ins
                P.add("pe", f, reads=[FT, FT + "x", TWN], writes=["b%d" % bk])
                for gl in range(2):
                    src = bank(bk)[:, gl * 256:(gl + 1) * 256].rearrange("p (a r k) -> p a r k", a=2, r=2)
                    dst = YT[:, gl, :, :, 2 * bb:2 * bb + 2].rearrange("p k r a -> p a r k")
                    if w_ == 0:
                        P.add("act", lambda e, src=src, dst=dst: e.activation(out=dst, in_=src, func=AF.Copy),
                              reads=["b%d" % bk], writes=["YT"])
                    else:
                        P.add("dve", lambda e, src=src, dst=dst: e.tensor_copy(out=dst, in_=src), reads=["b%d" % bk], writes=["YT"])

            def fft_B(gh, gl, kg):
                g = 2 * gh + gl
                idx = (gl * 8 + kg)
                w_ = idx % 2
                bt, bo, b2 = (3, 4, 6) if w_ == 0 else (0, 5, 7)
                tpb = bank_bf(bt)[:, 0:1024].rearrange("p (i c) -> p i c", i=8)
                YN, PN = "ytt%d" % w_, "pq%d" % w_

                def ft(e):
                    ins = None
                    for i in range(8):
                        ins = e.transpose(tpb[:, i, :], YT[:, gl, kg * 8 + i, :, :].rearrange("p r t -> p (r t)"), ident)
                    return ins
                P.add("pe", ft, reads=["YT"], writes=["b%d" % bt])
                if w_ == 0:
                    P.add("act", lambda e: e.activation(out=YTT[w_].rearrange("p i c -> p (i c)"), in_=bank_bf(bt)[:, 0:1024], func=AF.Copy),
                          reads=["b%d" % bt], writes=[YN])
                else:
                    P.add("dve", lambda e: e.tensor_copy(out=YTT[w_].rearrange("p i c -> p (i c)"), in_=bank_bf(bt)[:, 0:1024]),
                          reads=["b%d" % bt], writes=[YN])

                def fm(e):
                    ins = None
                    for i in range(8):
                        bk_ = bo if i < 4 else b2
                        ins = e.matmul(bank(bk_)[:, (i % 4) * 128:(i % 4 + 1) * 128], lhsT=YTT[w_][:, i, :], rhs=fbm,
                                       start=True, stop=True)
                    return ins
                P.add("pe", fm, reads=[YN], writes=["b%d" % bo, "b%d" % b2])
                for hh, bk_ in enumerate((bo, b2)):
                    src = bank(bk_).rearrange("p (i r k) -> p i r k", i=4, r=2)
                    dst = PQ[w_][:, hh * 4:(hh + 1) * 4, :, :]
                    if hh == 0:
                        P.add("act", lambda e, src=src, dst=dst: e.activation(out=dst, in_=src, func=AF.Copy),
                              reads=["b%d" % bk_], writes=[PN + "a"])
                    else:
                        P.add("dve", lambda e, src=src, dst=dst: e.tensor_copy(out=dst, in_=src), reads=["b%d" % bk_], writes=[PN + "b"])

                def f2(e):
                    e.matmul(bank(bo), lhsT=cs[:, 0:128], rhs=PQ[w_][:, :, 0, :], start=True, stop=False)
                    return e.matmul(bank(bo), lhsT=cs[:, 128:256], rhs=PQ[w_][:, :, 1, :], start=False, stop=True)
                P.add("pe", f2, reads=[PN + "a", PN + "b"], writes=["b%d" % bo])
                dst = cat_sb[:, g, :].rearrange("p (k2 k1) -> p k1 k2", k1=64)[:, kg * 8:(kg + 1) * 8, :]
                src = bank(bo).rearrange("p (i k) -> p i k", i=8)
                P.add("act", lambda e: e.activation(out=dst, in_=src, func=AF.Copy, scale=SCL), reads=["b%d" % bo], writes=["cat_sb"])

            for gh in range(2):
                P.begin_capture()
                for bb in range(32):
                    fft_A(gh, bb)
                fa_ops = P.end_capture()
                P.replay_merged(rec_ops if gh == 0 else [], fa_ops)
                groups = []
                for gl in range(2):
                    for kg in range(8):
                        P.begin_capture()
                        fft_B(gh, gl, kg)
                        cur = P.end_capture()
                        groups.append((cur[0:2], cur[2:5], cur[5:]))
                ng = len(groups)
                for i in range(ng + 2):
                    s1 = groups[i][0] if i < ng else []
                    s2 = groups[i - 1][1] if 0 <= i - 1 < ng else []
                    s3 = groups[i - 2][2] if 0 <= i - 2 < ng else []
                    m_ = []
                    a_ = b_ = 0
                    while a_ < len(s3) or b_ < len(s2):
                        if b_ >= len(s2) or (a_ < len(s3) and a_ * len(s2) <= b_ * len(s3)):
                            m_.append(s3[a_]); a_ += 1
                        else:
                            m_.append(s2[b_]); b_ += 1
                    P.replay_merged(m_, s1)
            P.barrier()
            A.release(mkw)
            A.off = mkf

            def pr_A(b, n):
                par, xi = load_norm_T(x_d[b, n * 128:(n + 1) * 128, :], gsA, shA, b)
                w_ = n % 2
                dma("sp", k_r2[w_].rearrange("p h d -> p (h d)"), kv_d[b, n * 128:(n + 1) * 128, 0:512], reads=["kv_d"],
                    writes=["rok%d" % w_], slot="ldk%d" % w_)
                dma("sp", v_b2[w_].rearrange("p h d -> p (h d)"), kv_d[b, n * 128:(n + 1) * 128, 512:1024], reads=["kv_d"],
                    writes=["v_b%d" % w_], slot="ldv%d" % w_)
                proj(par, 0, 1)
                proj(par, 1536, 2)
                rope(1, n, q_r2[w_], 0, "roq%d" % w_)
                P.add("act", lambda e: e.activation(out=sg2[w_], in_=bank(2), func=AF.Silu), reads=["b2"], writes=["sg%d" % w_])
                return xi

            def pr_BC(b, n, par):
                cf = cat_sb[:, :, n * 128:(n + 1) * 128]
                w_ = n % 2
                q_r, k_r, v_b, sg = q_r2[w_], k_r2[w_], v_b2[w_], sg2[w_]
                ROQ, ROK, VB, SG = "roq%d" % w_, "rok%d" % w_, "v_b%d" % w_, "sg%d" % w_
                hscale(q_f, q_r, 2, [ROQ], ["q_f"], "dve")
                hscale(q_b, q_r, 3, [ROQ], ["q_b"], "dve")
                hscale(k_z, k_r, 0, [ROK], ["k_z"])
                tq = bank_bf(5, 2).rearrange("p (i t) -> p i t", i=16)

                def ftq(e):
                    ins = None
                    for vi, src in enumerate((q_r, q_f, q_b, k_r)):
                        for h in range(4):
                            ins = e.transpose(tq[:, vi * 4 + h, :], src[:, h, :], ident)
                    return ins
                P.add("pe", ftq, reads=[ROQ, "q_f", "q_b", ROK], writes=["b5", "b6"])
                P.add("act", lambda e: e.activation(out=qkT[:, 0:8, :].rearrange("p i t -> p (i t)"), in_=bank_bf(5), func=AF.Copy),
                      reads=["b5"], writes=["qkTa"])
                P.add("dve", lambda e: e.tensor_copy(out=qkT[:, 8:16, :].rearrange("p i t -> p (i t)"), in_=bank_bf(6)),
                      reads=["b6"], writes=["qkTb"])

                def fsc(e):
                    ins = None
                    for h in range(4):
                        ins = e.matmul(bank(7)[:, h * 128:(h + 1) * 128], lhsT=qkT[:, 12 + h, :], rhs=qkT[:, h, :], start=True, stop=True)
                    return ins
                P.add("pe", fsc, reads=["qkTa", "qkTb"], writes=["b7"])
                P.add("dve", lambda e: e.tensor_tensor(out=PT.rearrange("p h c -> p (h c)"), in0=bank(7),
                                                       in1=dmask.rearrange("p h c -> p (h c)"), op=ALU.mult), reads=["b7"], writes=["PT"])
                Bn = B31 if n == 31 else KVB[:, n + 1, :]

                def fo(e):
                    ins = None
                    for h in range(4):
                        o_ = bank(5)[:, h * 128:(h + 1) * 128]
                        e.matmul(o_, lhsT=PT[:, h, :], rhs=v_b[:, h, :], start=True, stop=False)
                        e.matmul(o_, lhsT=qkT[:, 4 + h, :], rhs=Sfb[:, h * 128:(h + 1) * 128], start=False, stop=False)
                        ins = e.matmul(o_, lhsT=qkT[:, 8 + h, :], rhs=Bn[:, h * 128:(h + 1) * 128], start=False, stop=True)
                    return ins
                P.add("pe", fo, reads=["PT", VB, "Sfb", "qkTa", "qkTb"], writes=["b5"])
                kv_mm(6, k_z, True, True, ["k_z", VB], v_b)
                for h in range(4):
                    P.add("dve", lambda e, h=h: e.scalar_tensor_tensor(
                        out=Sf32[:, h * 128:(h + 1) * 128], in0=Sf32[:, h * 128:(h + 1) * 128], scalar=g128[:, h:h + 1],
                        in1=bank(6)[:, h * 128:(h + 1) * 128], op0=ALU.mult, op1=ALU.add), reads=["b6", "Sf32", "b5"], writes=["Sf32"])
                P.add("act", lambda e: e.activation(out=Sfb, in_=Sf32, func=AF.Copy), reads=["Sf32", "b5"], writes=["Sfb"])
                for h in range(4):
                    P.add("act", lambda e, h=h: e.activation(out=junk[:, 0:128], in_=bank(5)[:, h * 128:(h + 1) * 128], func=AF.Square,
                                                             accum_out=hs[:, h:h + 1]), reads=["b5"], writes=["hs", "junk"])
                rstd_ops(hs[:, 0:4], hs[:, 4:8], 1.0 / 128, ["hs"], ["hs"])
                for h in range(4):
                    P.add("dve", lambda e, h=h: e.scalar_tensor_tensor(
                        out=ret[:, h, :], in0=bank(5)[:, h * 128:(h + 1) * 128], scalar=hs[:, 4 + h:5 + h],
                        in1=sg[:, h * 128:(h + 1) * 128], op0=ALU.mult, op1=ALU.mult), reads=["b5", "hs", SG], writes=["ret"])
                split_at[0] = len(P.cap)
                tr = bank_bf(3)[:, 0:512].rearrange("p (h t) -> p h t", h=4)

                def ftr(e):
                    ins = None
                    for h in range(4):
                        ins = e.transpose(tr[:, h, :], ret[:, h, :], ident)
                    return ins
                P.add("pe", ftr, reads=["ret"], writes=["b3"])
                P.add("dve", lambda e: e.tensor_copy(out=catr.rearrange("p h c -> p (h c)"), in_=bank_bf(3)[:, 0:512]),
                      reads=["b3"], writes=["catr"])

                def fmix(e):
                    ins = None
                    for hh in range(2):
                        o_ = bank(3 + hh)
                        for k in range(8):
                            l_ = catr[:, k, :] if k < 4 else cf[:, k - 4, :]
                            ins = e.matmul(o_, lhsT=l_, rhs=wout[:, k, hh * 512:(hh + 1) * 512], start=(k == 0), stop=(k == 7))
                    return ins
                P.add("pe", fmix, reads=["catr"], writes=["b3", "b4"])
                X = "mx%d" % par
                S = "msq%d" % par
                P.add("act", lambda e: e.activation(out=junk, in_=bank(3, 2), func=AF.Square, accum_out=sq[par][:, 0:1]),
                      reads=["b3", "b4"], writes=[S, "junk"])
                rstd_ops(sq[par][:, 0:1], sq[par][:, 1:2], 1.0 / D, [S], [S])
                P.add("dve", lambda e: e.scalar_tensor_tensor(out=ytm, in0=bank(3, 2), scalar=sq[par][:, 1:2], in1=ggtA[:, b, :],
                                                              op0=ALU.mult, op1=ALU.mult), reads=["b3", "b4", S], writes=["ytm"])
                P.add("pool", lambda e: e.tensor_tensor(out=xtm[par], in0=ytm, in1=xtm[par], op=ALU.add), reads=["ytm", X], writes=[X, X + "s"])
                dma("pool", x1_d[b, n * 128:(n + 1) * 128, :], xtm[par], reads=[X, X + "s"], writes=["x1_d"], slot="st" + X)
            split_at = [0]
            par_next = pr_A(b, 0)
            lc_prev = []
            for n in range(32):
                par_cur = par_next
                la = []
                if n + 1 < 32:
                    P.begin_capture()
                    par_next = pr_A(b, n + 1)
                    la = P.end_capture()
                P.begin_capture()
                pr_BC(b, n, par_cur)
                lbc = P.end_capture()
                lb, lc = lbc[:split_at[0]], lbc[split_at[0]:]
                merged_bc = []
                i_ = j_ = 0
                while i_ < len(lc_prev) or j_ < len(lb):
                    if j_ >= len(lb) or (i_ < len(lc_prev) and i_ * len(lb) <= j_ * len(lc_prev)):
                        merged_bc.append(lc_prev[i_]); i_ += 1
                    else:
                        merged_bc.append(lb[j_]); j_ += 1
                P.replay_merged(la, merged_bc)
                lc_prev = lc
            P.replay_merged(lc_prev, [])
            P.barrier()
        if dbg is not None:
            import os
            what = os.environ.get("KDUMP", "KVB")
            srcs = {"KVB": KVB.rearrange("p n c -> p (n c)"), "cat": cat_sb.rearrange("p n c -> p (n c)")}
            A.release(mkw)
            t_ = A.alloc([4096], F32)
            for i_ in range(4):
                P.add("dve", lambda e, i_=i_: e.tensor_copy(out=t_, in_=srcs[what][:, i_ * 4096:(i_ + 1) * 4096]), writes=["dbgt"])
                dma("sp", dbg_d[:, i_ * 4096:(i_ + 1) * 4096], t_, reads=["dbgt"], writes=["dbgd"])
            P.barrier()
        A.release(mk)

    def ffn_phase(src_d):
        mk = A.mark()
        wup = A.alloc([8, 2 * DFF], BF16)
        wdn = A.alloc([22, D], BF16)
        wup_v = wup_d.rearrange("(k p) n -> p k n", p=128)
        for k in range(8):
            dma("pool", wup[:, k, :].rearrange("p (a c) -> p a c", c=1408),
                wup_v[:, k, :].rearrange("p (a c) -> p a c", c=1408), writes=["wup", "wq"])
        wdn_v = wdn_d.rearrange("(k p) n -> p k n", p=128)
        for k0 in range(0, 22, 11):
            dma("pool", wdn[:, k0:k0 + 11, :], wdn_v[:, k0:k0 + 11, :], writes=["wdn", "wq"])
        xt = [A.alloc([2, D], F32) for _ in range(2)]
        xn1 = A.alloc([2, D], BF16)
        xn = [xn1, xn1]
        junk = A.alloc([D], BF16)
        ssq = [A.alloc([4], F32) for _ in range(2)]
        hT = [A.alloc([8, 256], BF16) for _ in range(2)]
        LAG = 3
        acc = [A.alloc([256], F32) for _ in range(6)]
        sa = [A.alloc([256], F32) for _ in range(3)]
        gT = [A.alloc([256], BF16) for _ in range(LAG + 1)]
        ytmp = [A.alloc([D], F32) for _ in range(2)]
        st2 = [A.alloc([4], F32) for _ in range(2)]
        tiles = [(b, t0) for b in range(NB) for t0 in range(0, L, 256)]
        if stage == "ffn1":
            tiles = tiles[:2]

        def prelude(i):
            b, t0 = tiles[i]
            par = i % 2
            X, XN, S, H = "xt%d" % par, "xn", "ssq%d" % par, "hT%d" % par
            dma("sp", xt[par], src_d[b, t0:t0 + 256, :].rearrange("(s p) d -> p s d", p=128), writes=[X], slot=X)
            for s in range(2):
                P.add("act", lambda e, s=s: e.activation(out=junk, in_=xt[par][:, s, :], func=AF.Square,
                                                          accum_out=ssq[par][:, s:s + 1]),
                      reads=[X], writes=[S + "a%d" % s, "junk"])
            P.add("act", lambda e: e.activation(out=ssq[par][:, 2:4], in_=ssq[par][:, 0:2], func=AF.Sqrt,
                                                 bias=epsc[:, 0:1], scale=1.0 / D),
                  reads=[S + "a0", S + "a1"], writes=[S + "q"])
            P.add("dve", lambda e: e.reciprocal(out=ssq[par][:, 2:4], in_=ssq[par][:, 2:4]), reads=[S + "q"], writes=[S + "r"])
            for s in range(2):
                P.add("act", lambda e, s=s: e.activation(out=xn[par][:, s, :], in_=xt[par][:, s, :], func=AF.Copy,
                                                          scale=ssq[par][:, 2 + s:3 + s]),
                      reads=[X, S + "r"], writes=[XN + "_%d" % s])
            tpv = bank_bf(0, 2).rearrange("p (k t) -> p k t", k=8)

            def ftp(e):
                ins = None
                for s in range(2):
                    for k in range(8):
                        ins = e.transpose(tpv[:, k, s * 128:(s + 1) * 128], xn[par][:, s, k * 128:(k + 1) * 128], ident)
                return ins
            P.add("pe", ftp, reads=[XN + "_0", XN + "_1"], writes=["tp"])
            for k in range(8):
                if True:
                    P.add("dve", lambda e, k=k: e.tensor_scalar(out=hT[par][:, k, :], in0=tpv[:, k, :],
                                                                scalar1=gsF[:, k, b:b + 1], scalar2=shF[:, k, b:b + 1],
                                                                op0=ALU.mult, op1=ALU.add),
                          reads=["tp"], writes=[H + "_%d" % k])
                else:
                    P.add("act", lambda e, k=k: e.activation(out=hT[par][:, k, :], in_=tpv[:, k, :], func=AF.Identity,
                                                             scale=gsF[:, k, b:b + 1], bias=shF[:, k, b:b + 1]),
                          reads=["tp"], writes=[H + "_%d" % k])

        def up_pair(i, j):
            par = i % 2
            q = j % 2
            H = "hT%d" % par
            UB = "ub%d" % q

            def f(e):
                ins = None
                for half in range(2):
                    cc = j + 22 * half
                    uo = bank(2 + q)[:, half * 256:half * 256 + 256]
                    for k in range(8):
                        ins = e.matmul(uo, lhsT=wup[:, k, cc * 128:(cc + 1) * 128], rhs=hT[par][:, k, :],
                                       start=(k == 0), stop=(k == 7))
                return ins
            P.add("pe", f, reads=[H + "_%d" % k for k in range(8)] + ["wup"], writes=[UB])
            halves = []
            for half in range(2):
                cc = j + 22 * half
                slot = 2 * (j % 3) + half
                uo = bank(2 + q)[:, half * 256:half * 256 + 256]
                halves.append((cc, "acc%d" % slot, acc[slot], uo, uo.rearrange("p (r c) -> p r c", c=64),
                               acc[slot].rearrange("p (r c) -> p r c", c=64)))
            for (cc, AC, a, uo, u3, a3) in halves:
                P.add("dve", lambda e, a=a, uo=uo, cc=cc: e.tensor_scalar(out=a, in0=uo, scalar1=cwT[:, cc, 1:2],
                                                                          scalar2=cwT[:, cc, 3:4], op0=ALU.mult, op1=ALU.add),
                      reads=[UB], writes=[AC])
            for (cc, AC, a, uo, u3, a3) in halves:
                P.add("dve", lambda e, a3=a3, u3=u3, cc=cc: e.scalar_tensor_tensor(
                    out=a3[:, :, 1:64], in0=u3[:, :, 0:63], scalar=cwT[:, cc, 0:1], in1=a3[:, :, 1:64],
                    op0=ALU.mult, op1=ALU.add), reads=[UB, AC], writes=[AC])
            for (cc, AC, a, uo, u3, a3) in halves:
                P.add("dve", lambda e, a3=a3, u3=u3, cc=cc: e.scalar_tensor_tensor(
                    out=a3[:, :, 0:63], in0=u3[:, :, 1:64], scalar=cwT[:, cc, 2:3], in1=a3[:, :, 0:63],
                    op0=ALU.mult, op1=ALU.add), reads=[UB, AC], writes=[AC])

        def gate(j):
            q3 = j % 3
            sl = 2 * q3
            qg = j % (LAG + 1)
            P.add("act", lambda e: e.activation(out=sa[q3], in_=acc[sl], func=AF.Silu), reads=["acc%d" % sl], writes=["sa%d" % q3])
            P.add("pool", lambda e: e.tensor_tensor(out=gT[qg], in0=sa[q3], in1=acc[sl + 1], op=ALU.mult),
                  reads=["sa%d" % q3, "acc%d" % (sl + 1)], writes=["gT%d" % qg])

        def down(j):
            q = j % (LAG + 1)

            def f(e):
                ins = None
                for s in range(2):
                    for h in range(2):
                        ins = e.matmul(bank(4 + s * 2 + h), lhsT=gT[q][:, s * 128:(s + 1) * 128],
                                       rhs=wdn[:, j, h * 512:(h + 1) * 512], start=(j == 0), stop=(j == 21))
                return ins
            P.add("pe", f, reads=["gT%d" % q, "wdn"], writes=["dn"])

        def finale(i):
            b, t0 = tiles[i]
            par = i % 2
            X, Y, S2 = "xt%d" % par, "yt%d" % par, "st2_%d" % par
            for s in range(2):
                P.add("act", lambda e, s=s: e.activation(out=junk, in_=bank(4 + 2 * s, 2), func=AF.Square,
                                                          accum_out=st2[par][:, s:s + 1]),
                      reads=["dn"], writes=[S2 + "a%d" % s, "junk"])
            P.add("act", lambda e: e.activation(out=st2[par][:, 2:4], in_=st2[par][:, 0:2], func=AF.Sqrt, bias=epsc[:, 0:1], scale=1.0 / D),
                  reads=[S2 + "a0", S2 + "a1"], writes=[S2 + "q"])
            P.add("dve", lambda e: e.reciprocal(out=st2[par][:, 2:4], in_=st2[par][:, 2:4]), reads=[S2 + "q"], writes=[S2 + "r"])
            for s in range(2):
                P.add("dve", lambda e, s=s: e.scalar_tensor_tensor(out=ytmp[s], in0=bank(4 + 2 * s, 2),
                                                                   scalar=st2[par][:, 2 + s:3 + s], in1=ggtF[:, b, :],
                                                                   op0=ALU.mult, op1=ALU.mult),
                      reads=["dn", S2 + "r"], writes=["ytmp%d" % s])
                P.add("pool", lambda e, s=s: e.tensor_tensor(out=xt[par][:, s, :], in0=ytmp[s], in1=xt[par][:, s, :],
                                                             op=ALU.add),
                      reads=["ytmp%d" % s, X], writes=[X, X + "s"])
            dma("pool", out_d[b, t0:t0 + 256, :].rearrange("(s p) d -> p s d", p=128), xt[par],
                reads=[X, X + "s"], slot="st" + X)

        prelude(0)
        fin_ops = []
        for i in range(len(tiles)):
            P.begin_capture()
            for j in range(LAG):
                up_pair(i, j)
                gate(j)
            head = P.end_capture()
            P.replay_merged(fin_ops, head)
            pre_ops = []
            if i + 1 < len(tiles):
                P.begin_capture()
                prelude(i + 1)
                pre_ops = P.end_capture()
            P.begin_capture()
            for j in range(LAG, 22):
                up_pair(i, j)
                gate(j)
                down(j - LAG)
            mid = P.end_capture()
            P.replay_merged(pre_ops, mid)
            for j in range(22 - LAG, 22):
                down(j)
            P.begin_capture()
            finale(i)
            fin_ops = P.end_capture()
        P.replay_merged(fin_ops, [])
        P.barrier()
        A.release(mk)

    if stage in ("mix", "mix1"):
        mixer_phase()
        P.emit()
        es.close()
        return nc
    if stage == "full":
        mixer_phase()
        ffn_phase(x1_d)
        P.emit()
        es.close()
        return nc
    if stage in ("ffn", "ffn1"):
        ffn_phase(x_d)
        P.emit()
        es.close()
        return nc

    raise NotImplementedError(stage)


def make_in_maps(inputs):
    f32 = np.float32
    x = np.asarray(inputs["x"], f32)
    c = np.asarray(inputs["c"], f32)
    ctx = np.asarray(inputs["ctx"], f32)
    c_ctx = np.asarray(inputs["c_ctx"], f32)
    hc = host_consts()
    b_ada = np.asarray(inputs["b_ada"], f32)[0]
    gs = [np.asarray(inputs[k], f32)[0] for k in ("g_mix_pre", "g_mix_post", "g_ffn_pre", "g_ffn_post")]
    conv_w = np.asarray(inputs["conv_w"], f32)[0]
    conv_b = np.asarray(inputs["conv_b"], f32)[0]
    cw = np.concatenate([conv_w, conv_b[None]], axis=0)
    shared = {
        "w_ada": np.ascontiguousarray(np.asarray(inputs["w_ada"], f32)[0]),
        "b_ada_row": np.ascontiguousarray(b_ada[None, :]),
        "b_adaT": np.ascontiguousarray(b_ada.reshape(48, 128).T),
        "gT": np.ascontiguousarray(np.stack([g.reshape(8, 128).T for g in gs], axis=1)),
        "g_row": np.ascontiguousarray(np.stack(gs, axis=0)),
        "w_in": np.ascontiguousarray(np.asarray(inputs["w_in"], f32)[0]),
        "decay": np.ascontiguousarray(np.concatenate([np.asarray(inputs["ret_decay_fwd"], f32)[0],
                                                      np.asarray(inputs["ret_decay_bwd"], f32)[0]])[None, :]),
        "w_out": np.ascontiguousarray(np.asarray(inputs["w_out"], f32)[0]),
        "w_up": np.ascontiguousarray(np.asarray(inputs["w_up"], f32)[0]),
        "cwT": np.ascontiguousarray(cw.reshape(4, 44, 128).transpose(2, 1, 0)),
        "w_down": np.ascontiguousarray(np.asarray(inputs["w_down"], f32)[0]),
    }
    shared.update(hc)
    maps = []
    for i in range(NCORES):
        m = dict(shared)
        m["x"] = np.ascontiguousarray(x[NB * i:NB * (i + 1)])
        m["ctx"] = np.ascontiguousarray(ctx[NB * i:NB * (i + 1)])
        cc = np.stack([c[NB * i], c[NB * i + 1], c_ctx], axis=0)
        m["cT"] = np.ascontiguousarray(cc.reshape(3, 8, 128).transpose(2, 1, 0))
        maps.append(m)
    return maps


_NC_CACHE = {}


def kernel(**inputs):
    maps = make_in_maps(inputs)
    if "full" not in _NC_CACHE:
        _NC_CACHE["full"] = build("full")
    nc = _NC_CACHE["full"]
    res = run_bass_kernel_spmd(nc, maps, core_ids=list(range(NCORES)))
    return np.concatenate([np.asarray(r["out"]) for r in res.results], axis=0).astype(np.float32)
```

```python
import math
from contextlib import ExitStack

import numpy as np
import ml_dtypes
import concourse.bass as bass
import concourse.mybir as mybir
from concourse.bass_utils import run_bass_kernel_spmd

F32 = mybir.dt.float32
BF16 = mybir.dt.bfloat16
U8 = mybir.dt.uint8
AF = mybir.ActivationFunctionType
ALU = mybir.AluOpType

NCORES = 8
D = 1024
L = 4096
NB = 2
CTX = 256
DFF = 2816
INW = 2560
EPS = 1e-6
ENGS = ("pe", "act", "dve", "pool", "sp")


class Op:
    __slots__ = ("eng", "fn", "deps", "needed", "is_dma", "slot", "sem", "val")


class Prog:
    def __init__(self, nc):
        self.nc = nc
        self.ops = {e: [] for e in ENGS}
        self.res = {}
        self.bar = {e: [] for e in ENGS}
        self.slot_last = {}

    def begin_capture(self):
        self.cap = []

    def end_capture(self):
        c, self.cap = self.cap, None
        return c

    def replay_merged(self, la, lb):
        i = j = 0
        while i < len(la) or j < len(lb):
            if j >= len(lb) or (i < len(la) and i * len(lb) <= j * len(la)):
                self.add(*la[i])
                i += 1
            else:
                self.add(*lb[j])
                j += 1

    def add(self, eng, fn, reads=(), writes=(), slot=None):
        import os
        if getattr(self, "cap", None) is not None:
            self.cap.append((eng, fn, tuple(reads), tuple(writes), slot))
            return None
        self.nadd = getattr(self, "nadd", 0) + 1
        cut = int(os.environ.get("KCUT", "0"))
        if cut and self.nadd > cut and fn is not None and not os.environ.get("KCUT_OFF"):
            return None
        op = Op()
        op.eng, op.fn, op.needed, op.slot = eng, fn, False, slot
        op.is_dma = slot is not None
        op.sem = op.val = None
        hard, war = set(), set()
        for r in reads:
            st = self.res.get(r)
            if st is not None and st[0] is not None:
                hard.add(st[0])
        for w in writes:
            st = self.res.get(w)
            if st is not None:
                if st[0] is not None:
                    hard.add(st[0])
                for rd in st[1]:
                    war.add(rd)
        deps = set(self.bar[eng])
        self.bar[eng] = []
        for d in hard:
            if d.eng == eng and not d.is_dma and not op.is_dma and eng == "pe":
                continue
            deps.add(d)
        for d in war:
            if d.eng == eng and not d.is_dma and not op.is_dma and eng == "pe":
                continue
            deps.add(d)
        deps.discard(op)
        for d in deps:
            d.needed = True
        op.deps = list(deps)
        for r in reads:
            st = self.res.setdefault(r, [None, []])
            st[1].append(op)
        for w in writes:
            self.res[w] = [op, []]
        self.ops[eng].append(op)
        if op.is_dma:
            self.slot_last[slot] = op
        return op

    def barrier(self):
        lasts = [self.ops[e][-1] for e in ENGS if self.ops[e]]
        lasts += list(self.slot_last.values())
        for e in ENGS:
            self.bar[e] = list(lasts)
        self.res = {}

    def emit(self):
        nc = self.nc
        self.barrier()
        self.add("sp", None)
        es = ExitStack()
        sems = {e: es.enter_context(nc.semaphore("s_" + e)) for e in ENGS}
        slot_sem = {}
        for e in ENGS:
            n = 0
            for op in self.ops[e]:
                if op.is_dma:
                    if op.slot not in slot_sem:
                        slot_sem[op.slot] = [es.enter_context(nc.semaphore("d_" + str(len(slot_sem)))), 0]
                    ss = slot_sem[op.slot]
                    ss[1] += 16
                    op.sem, op.val = ss[0], ss[1]
                elif op.needed:
                    n += 1
                    op.sem, op.val = sems[e], n

        def run(e, eng):
            seen = {}
            for op in self.ops[e]:
                want = {}
                for d in op.deps:
                    assert d.sem is not None
                    k = d.sem.name
                    if want.get(k, (None, 0))[1] < d.val:
                        want[k] = (d.sem, d.val)
                for k, (s, v) in want.items():
                    if seen.get(k, 0) < v:
                        eng.wait_ge(s, v)
                        seen[k] = v
                if op.fn is not None:
                    ins = op.fn(eng)
                    if op.is_dma:
                        ins.then_inc(op.sem, 16)
                    elif op.needed:
                        ins.then_inc(op.sem, 1)

        with nc.Block() as block:
            @block.sync
            def _(eng):
                run("sp", eng)

            @block.tensor
            def _(eng):
                run("pe", eng)

            @block.scalar
            def _(eng):
                run("act", eng)

            @block.vector
            def _(eng):
                run("dve", eng)

            @block.gpsimd
            def _(eng):
                run("pool", eng)
        es.close()


class Arena:
    def __init__(self, base_u8, size):
        self.base, self.size, self.off = base_u8, size, 0

    def alloc(self, free_shape, dtype, align=64):
        isz = 2 if dtype == BF16 else 4
        n = int(np.prod(free_shape)) * isz
        off = (self.off + align - 1) // align * align
        assert off + n <= self.size, ("SBUF arena overflow", off + n, self.size)
        self.off = off + n
        v = self.base[:, off:off + n].bitcast(dtype)
        if len(free_shape) > 1:
            names = " ".join("a%d" % i for i in range(len(free_shape)))
            kw = {"a%d" % i: int(s) for i, s in enumerate(free_shape)}
            v = v.rearrange("p (%s) -> p %s" % (names, names), **kw)
        return v

    def mark(self):
        return self.off

    def release(self, m):
        self.off = m


def _bf(a):
    return np.ascontiguousarray(a.astype(np.float32)).astype(ml_dtypes.bfloat16)


def host_consts():
    c = {}
    c["ident"] = _bf(np.eye(128))
    t = np.arange(L)
    row = (t // 64).astype(np.float64)
    col = (t % 64).astype(np.float64)
    freqs = 10000.0 ** (-np.arange(32, dtype=np.float64) / 32)
    ang = np.concatenate([row[:, None] * freqs, col[:, None] * freqs], axis=-1)
    sc_ = 128.0 ** -0.5
    rt = np.stack([np.cos(ang), np.sin(ang)], axis=0)
    c["rope"] = _bf(rt.reshape(2, 32, 128, 64).transpose(2, 0, 1, 3))
    tw = np.zeros((32, 128, 256), np.float64)
    k1 = np.arange(64)[None, :].astype(np.float64)
    t1 = np.arange(64)[:, None].astype(np.float64)
    for b_ in range(32):
        for aa in range(2):
            ph = 2 * np.pi * k1 * (64 * t1 + 2 * b_ + aa) / 4096.0
            tw[b_, aa * 64:(aa + 1) * 64, aa * 128:aa * 128 + 64] = np.cos(ph)
            tw[b_, aa * 64:(aa + 1) * 64, aa * 128 + 64:aa * 128 + 128] = -np.sin(ph)
    c["tw"] = _bf(tw)
    t2 = np.arange(64)[:, None].astype(np.float64)
    k2 = np.arange(64)[None, :].astype(np.float64)
    th = 2 * np.pi * t2 * k2 / 64.0
    c["fb"] = _bf(np.concatenate([np.concatenate([np.cos(th), -np.sin(th)], axis=1),
                                  np.concatenate([np.sin(th), np.cos(th)], axis=1)], axis=0))
    m = np.arange(128)[:, None].astype(np.float64)
    cc = np.arange(128)[None, :].astype(np.float64)
    dk = np.zeros((128, 4, 128), np.float32)
    dk[:, 0] = np.maximum(cc - m, 0)
    dk[:, 1] = np.maximum(m - cc, 0)
    dk[:, 2] = (cc >= m)
    dk[:, 3] = (cc < m)
    c["dtab"] = dk
    pm = np.arange(128, dtype=np.float32)
    c["pcol"] = np.stack([127 - pm, pm, pm + 1, 128 - pm, 255 - pm, 127 - pm, pm, 128 + pm], axis=1).astype(np.float32)
    j = np.arange(128)[None, :].astype(np.float64)
    ch = np.arange(128)[:, None].astype(np.float64)
    a = 2 * np.pi * ch * j / 128
    c["cs"] = _bf(np.concatenate([np.cos(a), np.sin(a)], axis=1))
    return c


def build(stage="full", dbg=None):
    nc = bass.Bass("TRN2", target_bir_lowering=False)
    P = Prog(nc)

    def din(name, shape, dt=F32):
        return nc.dram_tensor(name, list(shape), dt, kind="ExternalInput").ap()

    x_d = din("x", [NB, L, D])
    ctx_d = din("ctx", [NB, CTX, D])
    cT_d = din("cT", [128, 8, 3])
    wada_d = din("w_ada", [D, 6 * D])
    bada_r_d = din("b_ada_row", [1, 6 * D])
    badaT_d = din("b_adaT", [128, 48])
    gT_d = din("gT", [128, 4, 8])
    grow_d = din("g_row", [4, D])
    win_d = din("w_in", [D, INW])
    dec_d = din("decay", [1, 8])
    wout_d = din("w_out", [D, D])
    wup_d = din("w_up", [D, 2 * DFF])
    cw_d = din("cwT", [128, 44, 4])
    wdn_d = din("w_down", [DFF, D])
    ident_d = din("ident", [128, 128], BF16)
    rope_d = din("rope", [128, 2, 32, 64], BF16)
    tw_d = din("tw", [32, 128, 256], BF16)
    fb_d = din("fb", [128, 128], BF16)
    dtab_d = din("dtab", [128, 4, 128])
    pcol_d = din("pcol", [128, 8])
    cs_d = din("cs", [128, 256], BF16)
    out_d = nc.dram_tensor("out", [NB, L, D], F32, kind="ExternalOutput").ap()
    x1_d = nc.dram_tensor("x1s", [NB, L, D], F32).ap()
    f_d = nc.dram_tensor("fscr", [NB, L, 512], BF16).ap()
    kv_d = nc.dram_tensor("kvscr", [NB, L, 1024], BF16).ap()
    dbg_d = None
    if dbg is not None:
        dbg_d = nc.dram_tensor("dbg", list(dbg), F32, kind="ExternalOutput").ap()

    es = ExitStack()
    ARENA_BYTES = 212800
    arena_t = es.enter_context(nc.sbuf_tensor("arena", [128, ARENA_BYTES], U8))
    A = Arena(arena_t, ARENA_BYTES)
    ps = es.enter_context(nc.psum_tensor("ps", [128, 4096], F32))

    def bank(i, n=1):
        return ps[:, i * 512:(i + n) * 512]

    def bank_bf(i, n=1):
        return ps[:, i * 512:(i + n) * 512].bitcast(BF16)

    ident = A.alloc([128], BF16)
    modT = A.alloc([4, 8, 3], F32)
    gsA = A.alloc([8, 3], F32)
    shA = A.alloc([8, 3], F32)
    gsF = A.alloc([8, 3], F32)
    shF = A.alloc([8, 3], F32)
    ggtA = A.alloc([NB, D], F32)
    ggtF = A.alloc([NB, D], F32)
    cwT = A.alloc([44, 4], F32)
    gTt = A.alloc([4, 8], F32)
    badaT = A.alloc([48], F32)
    scT = A.alloc([8, 3], BF16)
    cT = A.alloc([8, 3], F32)
    epsc = A.alloc([4], F32)

    ld = [0]

    def dma(eng, out, in_, reads=(), writes=(), slot=None):
        if slot is None:
            ld[0] += 1
            slot = "once%d" % ld[0]
        return P.add(eng, lambda e: e.dma_start(out=out, in_=in_), reads=reads, writes=writes, slot=slot)

    dma("sp", ident, ident_d, writes=["ident"])
    P.add("dve", lambda e: e.memset(epsc[:, 0:1], EPS), writes=["epsc"])
    P.add("dve", lambda e: e.memset(epsc[:, 1:2], 1.0), writes=["epsc"])
    P.add("dve", lambda e: e.memset(epsc[:, 2:3], -0.5), writes=["epsc"])
    dma("sp", cT, cT_d, writes=["cT"])
    dma("sp", cwT, cw_d, writes=["cwT"])
    dma("sp", gTt, gT_d, writes=["gTt"])
    dma("sp", badaT, badaT_d, writes=["badaT"])

    mk_ada = A.mark()
    wa = [A.alloc([8, D], BF16) for _ in range(2)]
    scR = A.alloc([8, NB, 128], BF16)
    rowt = A.alloc([D], F32)
    rowg = A.alloc([D], F32)
    P.add("act", lambda e: e.activation(out=scT, in_=cT, func=AF.Silu), reads=["cT"], writes=["scT"])
    for b in range(NB):
        P.add("dve", lambda e, b=b: e.tensor_copy(out=scR[:, :, b, :], in_=scT[:, :, b:b + 1].broadcast_to([128, 8, 128])),
              reads=["scT"], writes=["scR"])
    wada_v = wada_d.rearrange("(k p) n -> p k n", p=128)
    mods_fm = {0: 0, 1: 1, 3: 2, 4: 3}
    for m in range(6):
        w = wa[m % 2]
        wk = "wa%d" % (m % 2)
        dma("pool", w, wada_v[:, :, m * D:(m + 1) * D], writes=[wk, "wq"], slot=wk)
        if m in mods_fm:
            mi = mods_fm[m]
            def f(e, w=w):
                ins = None
                for kd in range(8):
                    for k in range(8):
                        ins = e.matmul(bank(0)[:, kd * 4:kd * 4 + 3], lhsT=w[:, k, kd * 128:(kd + 1) * 128],
                                       rhs=scT[:, k, :], start=(k == 0), stop=(k == 7))
                return ins
            P.add("pe", f, reads=[wk, "scT"], writes=["psA"])
            P.add("dve", lambda e, mi=mi, m=m: e.tensor_tensor(
                out=modT[:, mi], in0=bank(0)[:, 0:32].rearrange("p (k r) -> p k r", r=4)[:, :, 0:3],
                in1=badaT[:, m * 8:(m + 1) * 8].unsqueeze(2).broadcast_to([128, 8, 3]), op=ALU.add),
                reads=["psA", "badaT"], writes=["modT"])
        else:
            gi = 0 if m == 2 else 1
            tab = ggtA if m == 2 else ggtF
            dma("sp", rowt, bada_r_d[0:1, m * D:(m + 1) * D].partition_broadcast(128)[:, 0, :], writes=["rowt"], slot="rowt")
            dma("sp", rowg, grow_d[(1 if m == 2 else 3):(2 if m == 2 else 4), :].partition_broadcast(128)[:, 0, :],
                writes=["rowg"], slot="rowg")
            for b in range(NB):
                def f(e, w=w, b=b):
                    ins = None
                    for h in range(2):
                        for k in range(8):
                            ins = e.matmul(bank(1 + h), lhsT=scR[:, k, b, :], rhs=w[:, k, h * 512:(h + 1) * 512],
                                           start=(k == 0), stop=(k == 7))
                    return ins
                P.add("pe", f, reads=[wk, "scR"], writes=["psG"])
                P.add("dve", lambda e, tab=tab, b=b: e.tensor_tensor(out=tab[:, b, :], in0=bank(1, 2), in1=rowt, op=ALU.add),
                      reads=["psG", "rowt"], writes=["ggt"])
                P.add("dve", lambda e, tab=tab, b=b: e.tensor_tensor(out=tab[:, b, :], in0=tab[:, b, :], in1=rowg, op=ALU.mult),
                      reads=["ggt", "rowg"], writes=["ggt"])
    for (gs, sh, gi, m_sh, m_sc) in ((gsA, shA, 0, 0, 1), (gsF, shF, 2, 2, 3)):
        P.add("dve", lambda e, gs=gs, gi=gi, m_sc=m_sc: e.scalar_tensor_tensor(
            out=gs, in0=modT[:, m_sc], scalar=1.0, in1=gTt[:, gi, :].unsqueeze(2).broadcast_to([128, 8, 3]),
            op0=ALU.add, op1=ALU.mult), reads=["modT", "gTt"], writes=["gs"])
        P.add("dve", lambda e, sh=sh, m_sh=m_sh: e.tensor_copy(out=sh, in_=modT[:, m_sh]), reads=["modT"], writes=["sh"])
    P.barrier()
    A.release(mk_ada)

    if stage == "ada":
        t = A.alloc([24 + 24 + 2 * D], F32)
        P.add("dve", lambda e: e.tensor_copy(out=t[:, 0:24], in_=gsF.rearrange("p k r -> p (k r)")))
        P.add("dve", lambda e: e.tensor_copy(out=t[:, 24:48], in_=shF.rearrange("p k r -> p (k r)")))
        P.add("dve", lambda e: e.tensor_copy(out=t[:, 48:], in_=ggtF.rearrange("p b d -> p (b d)")), writes=["t"])
        dma("sp", dbg_d, t, reads=["t"])
        P.emit()
        es.close()
        return nc

    def mixer_phase():
        mk = A.mark()
        win = A.alloc([8, INW], BF16)
        wout = A.alloc([8, D], BF16)
        win_v = win_d.rearrange("(k p) n -> p k n", p=128)
        for k in range(8):
            dma("pool", win[:, k, :].rearrange("p (a c) -> p a c", c=1280),
                win_v[:, k, :].rearrange("p (a c) -> p a c", c=1280), writes=["win", "wq"])
        dma("pool", wout, wout_d.rearrange("(k p) n -> p k n", p=128), writes=["wout", "wq"])
        rc = A.alloc([2, 32, 64], BF16)
        dma("sp", rc, rope_d, writes=["rc"])
        cs = A.alloc([256], BF16)
        dma("sp", cs, cs_d, writes=["cs"])
        fbm = A.alloc([128], BF16)
        dma("sp", fbm, fb_d, writes=["fbm"])
        dtab = A.alloc([4, 128], F32)
        dma("sp", dtab, dtab_d, writes=["dtab"])
        pcol = A.alloc([8], F32)
        dma("sp", pcol, pcol_d, writes=["pcol"])
        dec = A.alloc([8], F32)
        dma("sp", dec, dec_d[0:1, :].partition_broadcast(128)[:, 0, :], writes=["dec"])
        lg = A.alloc([8], F32)
        ptab = A.alloc([8, 4], F32)
        g128 = A.alloc([8], F32)
        dmask = A.alloc([4, 128], F32)
        SQK = float(128 ** -0.5)
        tmpd = A.alloc([2, 128], F32)
        P.barrier()
        P.add("act", lambda e: e.activation(out=lg, in_=dec, func=AF.Exp, scale=-1.0), writes=["lg"])
        P.add("act", lambda e: e.activation(out=lg, in_=lg, func=AF.Ln, bias=epsc[:, 1:2]), reads=["lg"], writes=["lg"])
        P.add("dve", lambda e: e.tensor_scalar(out=lg, in0=lg, scalar1=-1.0, scalar2=None, op0=ALU.mult), reads=["lg"], writes=["lg"])
        isf = [True, False, True, False, True, True, False, False]
        for j in range(8):
            lo = 0 if isf[j] else 4
            P.add("dve", lambda e, j=j, lo=lo: e.tensor_scalar(out=ptab[:, j, :], in0=lg[:, lo:lo + 4], scalar1=pcol[:, j:j + 1],
                                                                scalar2=None, op0=ALU.mult), reads=["lg"], writes=["ptab"])
        P.add("act", lambda e: e.activation(out=ptab, in_=ptab, func=AF.Exp), reads=["ptab"], writes=["ptab"])
        P.add("act", lambda e: e.activation(out=g128, in_=lg, func=AF.Exp, scale=128.0), reads=["lg"], writes=["g128"])
        for h in range(4):
            P.add("act", lambda e, h=h: e.activation(out=tmpd[:, 0, :], in_=dtab[:, 0, :], func=AF.Exp, scale=lg[:, h:h + 1]),
                  reads=["lg", "dmask"], writes=["t0"])
            P.add("act", lambda e, h=h: e.activation(out=tmpd[:, 1, :], in_=dtab[:, 1, :], func=AF.Exp, scale=lg[:, 4 + h:5 + h]),
                  reads=["lg", "dmask"], writes=["t1"])
            P.add("dve", lambda e: e.scalar_tensor_tensor(out=tmpd[:, 0, :], in0=tmpd[:, 0, :], scalar=SQK, in1=dtab[:, 2, :],
                                                          op0=ALU.mult, op1=ALU.mult), reads=["t0"], writes=["t0"])
            P.add("dve", lambda e: e.scalar_tensor_tensor(out=tmpd[:, 1, :], in0=tmpd[:, 1, :], scalar=SQK, in1=dtab[:, 3, :],
                                                          op0=ALU.mult, op1=ALU.mult), reads=["t1"], writes=["t1"])
            P.add("dve", lambda e, h=h: e.tensor_tensor(out=dmask[:, h, :], in0=tmpd[:, 0, :], in1=tmpd[:, 1, :], op=ALU.add),
                  reads=["t0", "t1"], writes=["dmask"])
        P.add("dve", lambda e: e.tensor_scalar(out=ptab[:, 2:4, :], in0=ptab[:, 2:4, :], scalar1=SQK, scalar2=None, op0=ALU.mult),
              reads=["ptab"], writes=["ptab"])
        P.barrier()

        cat_sb = A.alloc([4, L], BF16)
        KVB = A.alloc([32, 512], BF16)
        Sf32 = A.alloc([512], F32)
        Sfb = A.alloc([512], BF16)
        Bst = A.alloc([512], F32)
        Bst_b = A.alloc([512], F32)
        B31 = A.alloc([512], BF16)
        mkw = A.mark()

        xtm = [A.alloc([D], F32) for _ in range(3)]
        xnm = A.alloc([D], BF16)
        junk = A.alloc([D], BF16)
        sq = [A.alloc([2], F32) for _ in range(3)]
        hTm = [A.alloc([8, 128], BF16) for _ in range(2)]
        ra = A.alloc([4, 64], F32)
        rb = A.alloc([4, 64], F32)
        q_r2 = [A.alloc([4, 128], BF16) for _ in range(2)]
        q_r = q_r2[0]
        q_f = A.alloc([4, 128], BF16)
        q_b = A.alloc([4, 128], BF16)
        k_r2 = [A.alloc([4, 128], BF16) for _ in range(2)]
        k_r = k_r2[0]
        k_z = A.alloc([4, 128], BF16)
        k_z2 = A.alloc([4, 128], BF16)
        v_b2 = [A.alloc([4, 128], BF16) for _ in range(2)]
        v_b = v_b2[0]
        sg2 = [A.alloc([512], BF16) for _ in range(2)]
        qkT = A.alloc([16, 128], BF16)
        PT = A.alloc([4, 128], BF16)
        ret = A.alloc([4, 128], BF16)
        catr = A.alloc([4, 128], BF16)
        fst = [A.alloc([512], BF16) for _ in range(2)]
        hs = A.alloc([8], F32)
        ytm = A.alloc([D], F32)
        tpv = bank_bf(0)[:, 0:1024].rearrange("p (k t) -> p k t", k=8)
        cnt = [0]

        def rstd_ops(src, dst, scale, rd, wr):
            P.add("dve", lambda e: e.tensor_scalar(out=dst, in0=src, scalar1=scale, scalar2=EPS, op0=ALU.mult, op1=ALU.add),
                  reads=rd, writes=wr)
            n_ = dst.shape[1]
            P.add("pool", lambda e: e.tensor_tensor(out=dst, in0=dst, in1=epsc[:, 2:3].broadcast_to([128, n_]), op=ALU.pow),
                  reads=wr, writes=wr)

        def load_norm_T(src_ap, gs, sh, r):
            i = cnt[0]
            cnt[0] += 1
            par = i % 2
            xi = i % 3
            X, S, H = "mx%d" % xi, "msq%d" % xi, "mh%d" % par
            dma("sp", xtm[xi], src_ap, writes=[X], slot=X)
            P.add("act", lambda e: e.activation(out=junk, in_=xtm[xi], func=AF.Square, accum_out=sq[xi][:, 0:1]),
                  reads=[X], writes=[S, "junk"])
            rstd_ops(sq[xi][:, 0:1], sq[xi][:, 1:2], 1.0 / D, [S], [S])
            P.add("dve", lambda e: e.tensor_scalar(out=xnm, in0=xtm[xi], scalar1=sq[xi][:, 1:2], scalar2=None, op0=ALU.mult),
                  reads=[X, S], writes=["xnm"])

            def ftp(e):
                ins = None
                for k in range(8):
                    ins = e.transpose(tpv[:, k, :], xnm[:, k * 128:(k + 1) * 128], ident)
                return ins
            P.add("pe", ftp, reads=["xnm"], writes=["b0"])
            for k in range(8):
                P.add("dve", lambda e, k=k: e.tensor_scalar(out=hTm[par][:, k, :], in0=tpv[:, k, :], scalar1=gs[:, k, r:r + 1],
                                                            scalar2=sh[:, k, r:r + 1], op0=ALU.mult, op1=ALU.add),
                      reads=["b0"], writes=[H])
            return par, xi

        def proj(par, col0, bi):
            def f(e):
                ins = None
                for k in range(8):
                    ins = e.matmul(bank(bi), lhsT=hTm[par][:, k, :], rhs=win[:, k, col0:col0 + 512], start=(k == 0), stop=(k == 7))
                return ins
            P.add("pe", f, reads=["mh%d" % par], writes=["b%d" % bi])

        def rope(bi, n, out, tb, RO="ro"):
            src = bank(bi).rearrange("p (h d) -> p h d", h=4)
            t1, t2 = src[:, :, 0:64], src[:, :, 64:128]
            cos = rc[:, tb, n, :].unsqueeze(1).broadcast_to([128, 4, 64])
            sin = rc[:, tb + 1, n, :].unsqueeze(1).broadcast_to([128, 4, 64])
            B_ = "b%d" % bi
            P.add("dve", lambda e: e.tensor_tensor(out=ra, in0=t1, in1=cos, op=ALU.mult), reads=[B_], writes=["ra"])
            P.add("dve", lambda e: e.tensor_tensor(out=rb, in0=t2, in1=sin, op=ALU.mult), reads=[B_], writes=["rb"])
            P.add("dve", lambda e: e.tensor_tensor(out=out[:, :, 0:64], in0=ra, in1=rb, op=ALU.subtract), reads=["ra", "rb"], writes=[RO])
            P.add("dve", lambda e: e.tensor_tensor(out=ra, in0=t1, in1=sin, op=ALU.mult), reads=[B_, RO], writes=["ra"])
            P.add("dve", lambda e: e.tensor_tensor(out=rb, in0=t2, in1=cos, op=ALU.mult), reads=[B_, RO], writes=["rb"])
            P.add("dve", lambda e: e.tensor_tensor(out=out[:, :, 64:128], in0=ra, in1=rb, op=ALU.add), reads=["ra", "rb"], writes=[RO])

        def hscale(out, in_, j, rd, wr, eng="pool"):
            P.add(eng, lambda e: e.tensor_tensor(out=out, in0=in_, in1=ptab[:, j, :].unsqueeze(2).broadcast_to([128, 4, 128]),
                                                 op=ALU.mult), reads=rd, writes=wr)

        def kv_mm(bi, kz, start, stop, rd, vv=None):
            vv = v_b if vv is None else vv
            def f(e):
                ins = None
                for h in range(4):
                    ins = e.matmul(bank(bi)[:, h * 128:(h + 1) * 128], lhsT=kz[:, h, :], rhs=vv[:, h, :], start=start, stop=stop)
                return ins
            P.add("pe", f, reads=rd, writes=["b%d" % bi])

        for b in range(1 if stage == "mix1" else NB):
            def ctx_tile(b, t):
                par, _xi = load_norm_T(ctx_d[b, t * 128:(t + 1) * 128, :], gsA, shA, 2)
                proj(par, 512, 1)
                proj(par, 1024, 2)
                P.add("act", lambda e: e.activation(out=k_r.rearrange("p h d -> p (h d)"), in_=bank(1), func=AF.Copy),
                      reads=["b1"], writes=["k_r"])
                P.add("act", lambda e: e.activation(out=v_b.rearrange("p h d -> p (h d)"), in_=bank(2), func=AF.Copy),
                      reads=["b2"], writes=["v_b"])
                hscale(k_z, k_r, 4 + t, ["k_r"], ["k_z"])
                hscale(k_z2, k_r, 6 + t, ["k_r"], ["k_z2"])
                kv_mm(6, k_z, True, True, ["k_z", "v_b"])
                kv_mm(7, k_z2, True, True, ["k_z2", "v_b"])
                if t == 0:
                    P.add("dve", lambda e: e.tensor_copy(out=Sf32, in_=bank(6)), reads=["b6"], writes=["Sf32"])
                    P.add("dve", lambda e: e.tensor_copy(out=Bst, in_=bank(7)), reads=["b7"], writes=["Bst"])
                else:
                    P.add("dve", lambda e: e.tensor_tensor(out=Sf32, in0=bank(6), in1=Sf32, op=ALU.add), reads=["b6", "Sf32"], writes=["Sf32"])
                    P.add("dve", lambda e: e.tensor_tensor(out=Bst, in0=bank(7), in1=Bst, op=ALU.add), reads=["b7", "Bst"], writes=["Bst"])
            for t in range(2):
                ctx_tile(b, t)
            P.add("act", lambda e: e.activation(out=Sfb, in_=Sf32, func=AF.Copy), reads=["Sf32"], writes=["Sfb"])
            P.add("act", lambda e: e.activation(out=B31, in_=Bst, func=AF.Copy), reads=["Bst"], writes=["B31"])

            def p1_A(b, n):
                par, _xi = load_norm_T(x_d[b, n * 128:(n + 1) * 128, :], gsA, shA, b)
                base = 1 + 3 * (n % 2)
                proj(par, 2048, base)
                proj(par, 512, base + 1)
                proj(par, 1024, base + 2)

            def p1_B(b, n):
                base = 1 + 3 * (n % 2)
                fs_ = fst[n % 2]
                FS = "fst%d" % (n % 2)
                P.add("act", lambda e: e.activation(out=fs_, in_=bank(base), func=AF.Copy), reads=["b%d" % base], writes=[FS])
                dma("pool", f_d[b, n * 128:(n + 1) * 128, :], fs_, reads=[FS], writes=["f_d"], slot="st" + FS)
                w_ = n % 2
                kr_, vb_ = k_r2[w_], v_b2[w_]
                ROK, VB = "rok%d" % w_, "v_b%d" % w_
                rope(base + 1, n, kr_, 0, ROK)
                P.add("act", lambda e: e.activation(out=vb_.rearrange("p h d -> p (h d)"), in_=bank(base + 2), func=AF.Copy),
                      reads=["b%d" % (base + 2)], writes=[VB])
                dma("pool", kv_d[b, n * 128:(n + 1) * 128, 0:512], kr_.rearrange("p h d -> p (h d)"), reads=[ROK], writes=["kv_d"],
                    slot="stk%d" % w_)
                dma("pool", kv_d[b, n * 128:(n + 1) * 128, 512:1024], vb_.rearrange("p h d -> p (h d)"), reads=[VB], writes=["kv_d"],
                    slot="stv%d" % w_)
                hscale(k_z, kr_, 1, [ROK], ["k_z"])
                kv_mm(7, k_z, True, True, ["k_z", VB], vb_)
                P.add("act", lambda e: e.activation(out=KVB[:, n, :], in_=bank(7), func=AF.Copy), reads=["b7"], writes=["KVB"])
            def merge2(la, lb):
                out, i_, j_ = [], 0, 0
                while i_ < len(la) or j_ < len(lb):
                    if j_ >= len(lb) or (i_ < len(la) and i_ * len(lb) <= j_ * len(la)):
                        out.append(la[i_]); i_ += 1
                    else:
                        out.append(lb[j_]); j_ += 1
                return out

            def cap_p1A(n):
                P.begin_capture()
                p1_A(b, n)
                ops = P.end_capture()
                return ops[:-3], ops[-3:]
            a1 = {}
            a2 = {}
            for n0 in (0, 1):
                a1[n0], a2[n0] = cap_p1A(n0)
            P.replay_merged(a1[0], [])
            P.replay_merged(a1[1], a2[0])
            for n in range(32):
                if n + 2 < 32:
                    a1[n + 2], a2[n + 2] = cap_p1A(n + 2)
                P.begin_capture()
                p1_B(b, n)
                lb = P.end_capture()
                P.replay_merged(a1.get(n + 2, []), merge2(a2.get(n + 1, []), lb))
            P.barrier()
            P.begin_capture()
            bsts = [Bst, Bst_b]
            for n in range(30, -1, -1):
                src_, dst_ = bsts[n % 2], bsts[(n + 1) % 2]
                SN, DN = "Bst%d" % (n % 2), "Bst%d" % ((n + 1) % 2)
                for h in range(4):
                    P.add("dve", lambda e, n=n, h=h, src_=src_, dst_=dst_: e.scalar_tensor_tensor(
                        out=dst_[:, h * 128:(h + 1) * 128], in0=src_[:, h * 128:(h + 1) * 128], scalar=g128[:, 4 + h:5 + h],
                        in1=KVB[:, n + 1, h * 128:(h + 1) * 128], op0=ALU.mult, op1=ALU.add),
                        reads=["KVB%d" % (n + 1), SN + "_%d" % h], writes=[DN + "_%d" % h])
                P.add("act", lambda e, n=n, dst_=dst_: e.activation(out=KVB[:, n + 1, :], in_=dst_, func=AF.Copy),
                      reads=[DN + "_%d" % h for h in range(4)], writes=["KVB%d" % (n + 1)])
            rec_ops = P.end_capture()

            mkf = A.mark()
            A.release(mkw)
            YT = A.alloc([2, 64, 2, 64], BF16)
            ftl = [A.alloc([512], BF16) for _ in range(2)]
            twl = [A.alloc([256], BF16) for _ in range(2)]
            YTT = [A.alloc([8, 128], BF16) for _ in range(2)]
            PQ = [A.alloc([8, 2, 64], BF16) for _ in range(2)]
            fview = f_d[b].rearrange("(t1 w) c -> w t1 c", w=64)
            SCL = float(1.0 / math.sqrt(L * 128.0))

            def fft_A(gh, bb):
                w_ = bb % 2
                FT, TWN = "ftl%d" % w_, "twl%d" % w_
                dma("sp", ftl[w_][0:64], fview[2 * bb], reads=["f_d"], writes=[FT], slot=FT + "a")
                dma("sp", ftl[w_][64:128], fview[2 * bb + 1], reads=["f_d"], writes=[FT + "x"], slot=FT + "b")
                dma("sp", twl[w_], tw_d[bb], writes=[TWN], slot=TWN)
                bk = 1 + w_

                def f(e):
                    ins = None
                    for gl in range(2):
                        g = 2 * gh + gl
                        ins = e.matmul(bank(bk)[:, gl * 256:(gl + 1) * 256], lhsT=ftl[w_][:, g * 128:(g + 1) * 128], rhs=twl[w_],
                                       start=True, stop=True)
                    return ins
                P.add("pe", f, reads=[FT, FT + "x", TWN], writes=["b%d" % bk])
                for gl in range(2):
                    src = bank(bk)[:, gl * 256:(gl + 1) * 256].rearrange("p (a r k) -> p a r k", a=2, r=2)
                    dst = YT[:, gl, :, :, 2 * bb:2 * bb + 2].rearrange("p k r a -> p a r k")
                    if w_ == 0:
                        P.add("act", lambda e, src=src, dst=dst: e.activation(out=dst, in_=src, func=AF.Copy),
                              reads=["b%d" % bk], writes=["YT"])
                    else:
                        P.add("dve", lambda e, src=src, dst=dst: e.tensor_copy(out=dst, in_=src), reads=["b%d" % bk], writes=["YT"])

            def fft_B(gh, gl, kg):
                g = 2 * gh + gl
                idx = (gl * 8 + kg)
                w_ = idx % 2
                bt, bo, b2 = (3, 4, 6) if w_ == 0 else (0, 5, 7)
                tpb = bank_bf(bt)[:, 0:1024].rearrange("p (i c) -> p i c", i=8)
                YN, PN = "ytt%d" % w_, "pq%d" % w_

                def ft(e):
                    ins = None
                    for i in range(8):
                        ins = e.transpose(tpb[:, i, :], YT[:, gl, kg * 8 + i, :, :].rearrange("p r t -> p (r t)"), ident)
                    return ins
                P.add("pe", ft, reads=["YT"], writes=["b%d" % bt])
                if w_ == 0:
                    P.add("act", lambda e: e.activation(out=YTT[w_].rearrange("p i c -> p (i c)"), in_=bank_bf(bt)[:, 0:1024], func=AF.Copy),
                          reads=["b%d" % bt], writes=[YN])
                else:
                    P.add("dve", lambda e: e.tensor_copy(out=YTT[w_].rearrange("p i c -> p (i c)"), in_=bank_bf(bt)[:, 0:1024]),
                          reads=["b%d" % bt], writes=[YN])

                def fm(e):
                    ins = None
                    for i in range(8):
                        bk_ = bo if i < 4 else b2
                        ins = e.matmul(bank(bk_)[:, (i % 4) * 128:(i % 4 + 1) * 128], lhsT=YTT[w_][:, i, :], rhs=fbm,
                                       start=True, stop=True)
                    return ins
                P.add("pe", fm, reads=[YN], writes=["b%d" % bo, "b%d" % b2])
                for hh, bk_ in enumerate((bo, b2)):
                    src = bank(bk_).rearrange("p (i r k) -> p i r k", i=4, r=2)
                    dst = PQ[w_][:, hh * 4:(hh + 1) * 4, :, :]
                    if hh == 0:
                        P.add("act", lambda e, src=src, dst=dst: e.activation(out=dst, in_=src, func=AF.Copy),
                              reads=["b%d" % bk_], writes=[PN + "a"])
                    else:
                        P.add("dve", lambda e, src=src, dst=dst: e.tensor_copy(out=dst, in_=src), reads=["b%d" % bk_], writes=[PN + "b"])

                def f2(e):
                    e.matmul(bank(bo), lhsT=cs[:, 0:128], rhs=PQ[w_][:, :, 0, :], start=True, stop=False)
                    return e.matmul(bank(bo), lhsT=cs[:, 128:256], rhs=PQ[w_][:, :, 1, :], start=False, stop=True)
                P.add("pe", f2, reads=[PN + "a", PN + "b"], writes=["b%d" % bo])
                dst = cat_sb[:, g, :].rearrange("p (k2 k1) -> p k1 k2", k1=64)[:, kg * 8:(kg + 1) * 8, :]
                src = bank(bo).rearrange("p (i k) -> p i k", i=8)
                P.add("act", lambda e: e.activation(out=dst, in_=src, func=AF.Copy, scale=SCL), reads=["b%d" % bo], writes=["cat_sb"])

            for gh in range(2):
                P.begin_capture()
                for bb in range(32):
                    fft_A(gh, bb)
                fa_ops = P.end_capture()
                P.replay_merged(rec_ops if gh == 0 else [], fa_ops)
                groups = []
                for gl in range(2):
                    for kg in range(8):
                        P.begin_capture()
                        fft_B(gh, gl, kg)
                        cur = P.end_capture()
                        groups.append((cur[0:2], cur[2:5], cur[5:]))
                ng = len(groups)
                for i in range(ng + 2):
                    s1 = groups[i][0] if i < ng else []
                    s2 = groups[i - 1][1] if 0 <= i - 1 < ng else []
                    s3 = groups[i - 2][2] if 0 <= i - 2 < ng else []
                    m_ = []
                    a_ = b_ = 0
                    while a_ < len(s3) or b_ < len(s2):
                        if b_ >= len(s2) or (a_ < len(s3) and a_ * len(s2) <= b_ * len(s3)):
                            m_.append(s3[a_]); a_ += 1
                        else:
                            m_.append(s2[b_]); b_ += 1
                    P.replay_merged(m_, s1)
            P.barrier()
            A.release(mkw)
            A.off = mkf

            def pr_A(b, n):
                par, xi = load_norm_T(x_d[b, n * 128:(n + 1) * 128, :], gsA, shA, b)
                w_ = n % 2
                dma("sp", k_r2[w_].rearrange("p h d -> p (h d)"), kv_d[b, n * 128:(n + 1) * 128, 0:512], reads=["kv_d"],
                    writes=["rok%d" % w_], slot="ldk%d" % w_)
                dma("sp", v_b2[w_].rearrange("p h d -> p (h d)"), kv_d[b, n * 128:(n + 1) * 128, 512:1024], reads=["kv_d"],
                    writes=["v_b%d" % w_], slot="ldv%d" % w_)
                proj(par, 0, 1)
                proj(par, 1536, 2)
                rope(1, n, q_r2[w_], 0, "roq%d" % w_)
                P.add("act", lambda e: e.activation(out=sg2[w_], in_=bank(2), func=AF.Silu), reads=["b2"], writes=["sg%d" % w_])
                return xi

            def pr_BC(b, n, par):
                cf = cat_sb[:, :, n * 128:(n + 1) * 128]
                w_ = n % 2
                q_r, k_r, v_b, sg = q_r2[w_], k_r2[w_], v_b2[w_], sg2[w_]
                ROQ, ROK, VB, SG = "roq%d" % w_, "rok%d" % w_, "v_b%d" % w_, "sg%d" % w_
                hscale(q_f, q_r, 2, [ROQ], ["q_f"], "dve")
                hscale(q_b, q_r, 3, [ROQ], ["q_b"], "dve")
                hscale(k_z, k_r, 0, [ROK], ["k_z"])
                tq = bank_bf(5, 2).rearrange("p (i t) -> p i t", i=16)

                def ftq(e):
                    ins = None
                    for vi, src in enumerate((q_r, q_f, q_b, k_r)):
                        for h in range(4):
                            ins = e.transpose(tq[:, vi * 4 + h, :], src[:, h, :], ident)
                    return ins
                P.add("pe", ftq, reads=[ROQ, "q_f", "q_b", ROK], writes=["b5", "b6"])
                P.add("act", lambda e: e.activation(out=qkT[:, 0:8, :].rearrange("p i t -> p (i t)"), in_=bank_bf(5), func=AF.Copy),
                      reads=["b5"], writes=["qkTa"])
                P.add("dve", lambda e: e.tensor_copy(out=qkT[:, 8:16, :].rearrange("p i t -> p (i t)"), in_=bank_bf(6)),
                      reads=["b6"], writes=["qkTb"])

                def fsc(e):
                    ins = None
                    for h in range(4):
                        ins = e.matmul(bank(7)[:, h * 128:(h + 1) * 128], lhsT=qkT[:, 12 + h, :], rhs=qkT[:, h, :], start=True, stop=True)
                    return ins
                P.add("pe", fsc, reads=["qkTa", "qkTb"], writes=["b7"])
                P.add("dve", lambda e: e.tensor_tensor(out=PT.rearrange("p h c -> p (h c)"), in0=bank(7),
                                                       in1=dmask.rearrange("p h c -> p (h c)"), op=ALU.mult), reads=["b7"], writes=["PT"])
                Bn = B31 if n == 31 else KVB[:, n + 1, :]

                def fo(e):
                    ins = None
                    for h in range(4):
                        o_ = bank(5)[:, h * 128:(h + 1) * 128]
                        e.matmul(o_, lhsT=PT[:, h, :], rhs=v_b[:, h, :], start=True, stop=False)
                        e.matmul(o_, lhsT=qkT[:, 4 + h, :], rhs=Sfb[:, h * 128:(h + 1) * 128], start=False, stop=False)
                        ins = e.matmul(o_, lhsT=qkT[:, 8 + h, :], rhs=Bn[:, h * 128:(h + 1) * 128], start=False, stop=True)
                    return ins
                P.add("pe", fo, reads=["PT", VB, "Sfb", "qkTa", "qkTb"], writes=["b5"])
                kv_mm(6, k_z, True, True, ["k_z", VB], v_b)
                for h in range(4):
                    P.add("dve", lambda e, h=h: e.scalar_tensor_tensor(
                        out=Sf32[:, h * 128:(h + 1) * 128], in0=Sf32[:, h * 128:(h + 1) * 128], scalar=g128[:, h:h + 1],
                        in1=bank(6)[:, h * 128:(h + 1) * 128], op0=ALU.mult, op1=ALU.add), reads=["b6", "Sf32", "b5"], writes=["Sf32"])
                P.add("act", lambda e: e.activation(out=Sfb, in_=Sf32, func=AF.Copy), reads=["Sf32", "b5"], writes=["Sfb"])
                for h in range(4):
                    P.add("act", lambda e, h=h: e.activation(out=junk[:, 0:128], in_=bank(5)[:, h * 128:(h + 1) * 128], func=AF.Square,
                                                             accum_out=hs[:, h:h + 1]), reads=["b5"], writes=["hs", "junk"])
                rstd_ops(hs[:, 0:4], hs[:, 4:8], 1.0 / 128, ["hs"], ["hs"])
                for h in range(4):
                    P.add("dve", lambda e, h=h: e.scalar_tensor_tensor(
                        out=ret[:, h, :], in0=bank(5)[:, h * 128:(h + 1) * 128], scalar=hs[:, 4 + h:5 + h],
                        in1=sg[:, h * 128:(h + 1) * 128], op0=ALU.mult, op1=ALU.mult), reads=["b5", "hs", SG], writes=["ret"])
                split_at[0] = len(P.cap)
                tr = bank_bf(3)[:, 0:512].rearrange("p (h t) -> p h t", h=4)

                def ftr(e):
                    ins = None
                    for h in range(4):
                        ins = e.transpose(tr[:, h, :], ret[:, h, :], ident)
                    return ins
                P.add("pe", ftr, reads=["ret"], writes=["b3"])
                P.add("dve", lambda e: e.tensor_copy(out=catr.rearrange("p h c -> p (h c)"), in_=bank_bf(3)[:, 0:512]),
                      reads=["b3"], writes=["catr"])

                def fmix(e):
                    ins = None
                    for hh in range(2):
                        o_ = bank(3 + hh)
                        for k in range(8):
                            l_ = catr[:, k, :] if k < 4 else cf[:, k - 4, :]
                            ins = e.matmul(o_, lhsT=l_, rhs=wout[:, k, hh * 512:(hh + 1) * 512], start=(k == 0), stop=(k == 7))
                    return ins
                P.add("pe", fmix, reads=["catr"], writes=["b3", "b4"])
                X = "mx%d" % par
                S = "msq%d" % par
                P.add("act", lambda e: e.activation(out=junk, in_=bank(3, 2), func=AF.Square, accum_out=sq[par][:, 0:1]),
                      reads=["b3", "b4"], writes=[S, "junk"])
                rstd_ops(sq[par][:, 0:1], sq[par][:, 1:2], 1.0 / D, [S], [S])
                P.add("dve", lambda e: e.scalar_tensor_tensor(out=ytm, in0=bank(3, 2), scalar=sq[par][:, 1:2], in1=ggtA[:, b, :],
                                                              op0=ALU.mult, op1=ALU.mult), reads=["b3", "b4", S], writes=["ytm"])
                P.add("pool", lambda e: e.tensor_tensor(out=xtm[par], in0=ytm, in1=xtm[par], op=ALU.add), reads=["ytm", X], writes=[X, X + "s"])
                dma("pool", x1_d[b, n * 128:(n + 1) * 128, :], xtm[par], reads=[X, X + "s"], writes=["x1_d"], slot="st" + X)
            split_at = [0]
            par_next = pr_A(b, 0)
            lc_prev = []
            for n in range(32):
                par_cur = par_next
                la = []
                if n + 1 < 32:
                    P.begin_capture()
                    par_next = pr_A(b, n + 1)
                    la = P.end_capture()
                P.begin_capture()
                pr_BC(b, n, par_cur)
                lbc = P.end_capture()
                lb, lc = lbc[:split_at[0]], lbc[split_at[0]:]
                merged_bc = []
                i_ = j_ = 0
                while i_ < len(lc_prev) or j_ < len(lb):
                    if j_ >= len(lb) or (i_ < len(lc_prev) and i_ * len(lb) < j_ * len(lc_prev)):
                        merged_bc.append(lc_prev[i_]); i_ += 1
                    else:
                        merged_bc.append(lb[j_]); j_ += 1
                P.replay_merged(merged_bc, la)
                lc_prev = lc
            P.replay_merged(lc_prev, [])
            P.barrier()
        if dbg is not None:
            import os
            what = os.environ.get("KDUMP", "KVB")
            srcs = {"KVB": KVB.rearrange("p n c -> p (n c)"), "cat": cat_sb.rearrange("p n c -> p (n c)")}
            A.release(mkw)
            t_ = A.alloc([4096], F32)
            for i_ in range(4):
                P.add("dve", lambda e, i_=i_: e.tensor_copy(out=t_, in_=srcs[what][:, i_ * 4096:(i_ + 1) * 4096]), writes=["dbgt"])
                dma("sp", dbg_d[:, i_ * 4096:(i_ + 1) * 4096], t_, reads=["dbgt"], writes=["dbgd"])
            P.barrier()
        A.release(mk)

    def ffn_phase(src_d):
        mk = A.mark()
        wup = A.alloc([8, 2 * DFF], BF16)
        wdn = A.alloc([22, D], BF16)
        wup_v = wup_d.rearrange("(k p) n -> p k n", p=128)
        for k in range(8):
            dma("pool", wup[:, k, :].rearrange("p (a c) -> p a c", c=1408),
                wup_v[:, k, :].rearrange("p (a c) -> p a c", c=1408), writes=["wup", "wq"])
        wdn_v = wdn_d.rearrange("(k p) n -> p k n", p=128)
        for k0 in range(0, 22, 11):
            dma("pool", wdn[:, k0:k0 + 11, :], wdn_v[:, k0:k0 + 11, :], writes=["wdn", "wq"])
        xt = [A.alloc([2, D], F32) for _ in range(2)]
        xn1 = A.alloc([2, D], BF16)
        xn = [xn1, xn1]
        junk = A.alloc([D], BF16)
        ssq = [A.alloc([4], F32) for _ in range(2)]
        hT = [A.alloc([8, 256], BF16) for _ in range(2)]
        LAG = 3
        acc = [A.alloc([256], F32) for _ in range(6)]
        sa = [A.alloc([256], F32) for _ in range(3)]
        gT = [A.alloc([256], BF16) for _ in range(LAG + 1)]
        ytmp = [A.alloc([D], F32) for _ in range(2)]
        st2 = [A.alloc([4], F32) for _ in range(2)]
        tiles = [(b, t0) for b in range(NB) for t0 in range(0, L, 256)]
        if stage == "ffn1":
            tiles = tiles[:2]

        def prelude(i):
            b, t0 = tiles[i]
            par = i % 2
            X, XN, S, H = "xt%d" % par, "xn", "ssq%d" % par, "hT%d" % par
            dma("sp", xt[par], src_d[b, t0:t0 + 256, :].rearrange("(s p) d -> p s d", p=128), writes=[X], slot=X)
            for s in range(2):
                P.add("act", lambda e, s=s: e.activation(out=junk, in_=xt[par][:, s, :], func=AF.Square,
                                                          accum_out=ssq[par][:, s:s + 1]),
                      reads=[X], writes=[S + "a%d" % s, "junk"])
            P.add("act", lambda e: e.activation(out=ssq[par][:, 2:4], in_=ssq[par][:, 0:2], func=AF.Sqrt,
                                                 bias=epsc[:, 0:1], scale=1.0 / D),
                  reads=[S + "a0", S + "a1"], writes=[S + "q"])
            P.add("dve", lambda e: e.reciprocal(out=ssq[par][:, 2:4], in_=ssq[par][:, 2:4]), reads=[S + "q"], writes=[S + "r"])
            for s in range(2):
                P.add("dve", lambda e, s=s: e.tensor_scalar(out=xn[par][:, s, :], in0=xt[par][:, s, :],
                                                            scalar1=ssq[par][:, 2 + s:3 + s], scalar2=None, op0=ALU.mult),
                      reads=[X, S + "r"], writes=[XN + "_%d" % s])
            tpv = bank_bf(0, 2).rearrange("p (k t) -> p k t", k=8)

            def ftp(e):
                ins = None
                for s in range(2):
                    for k in range(8):
                        ins = e.transpose(tpv[:, k, s * 128:(s + 1) * 128], xn[par][:, s, k * 128:(k + 1) * 128], ident)
                return ins
            P.add("pe", ftp, reads=[XN + "_0", XN + "_1"], writes=["tp"])
            for k in range(8):
                if True:
                    P.add("dve", lambda e, k=k: e.tensor_scalar(out=hT[par][:, k, :], in0=tpv[:, k, :],
                                                                scalar1=gsF[:, k, b:b + 1], scalar2=shF[:, k, b:b + 1],
                                                                op0=ALU.mult, op1=ALU.add),
                          reads=["tp"], writes=[H + "_%d" % k])
                else:
                    P.add("act", lambda e, k=k: e.activation(out=hT[par][:, k, :], in_=tpv[:, k, :], func=AF.Identity,
                                                             scale=gsF[:, k, b:b + 1], bias=shF[:, k, b:b + 1]),
                          reads=["tp"], writes=[H + "_%d" % k])

        def up_pair(i, j):
            par = i % 2
            q = j % 2
            H = "hT%d" % par
            UB = "ub%d" % q

            def f(e):
                ins = None
                for half in range(2):
                    cc = j + 22 * half
                    uo = bank(2 + q)[:, half * 256:half * 256 + 256]
                    for k in range(8):
                        ins = e.matmul(uo, lhsT=wup[:, k, cc * 128:(cc + 1) * 128], rhs=hT[par][:, k, :],
                                       start=(k == 0), stop=(k == 7))
                return ins
            P.add("pe", f, reads=[H + "_%d" % k for k in range(8)] + ["wup"], writes=[UB])
            halves = []
            for half in range(2):
                cc = j + 22 * half
                slot = 2 * (j % 3) + half
                uo = bank(2 + q)[:, half * 256:half * 256 + 256]
                halves.append((cc, "acc%d" % slot, acc[slot], uo, uo.rearrange("p (r c) -> p r c", c=64),
                               acc[slot].rearrange("p (r c) -> p r c", c=64)))
            for (cc, AC, a, uo, u3, a3) in halves:
                P.add("dve", lambda e, a=a, uo=uo, cc=cc: e.tensor_scalar(out=a, in0=uo, scalar1=cwT[:, cc, 1:2],
                                                                          scalar2=cwT[:, cc, 3:4], op0=ALU.mult, op1=ALU.add),
                      reads=[UB], writes=[AC])
            for (cc, AC, a, uo, u3, a3) in halves:
                P.add("dve", lambda e, a3=a3, u3=u3, cc=cc: e.scalar_tensor_tensor(
                    out=a3[:, :, 1:64], in0=u3[:, :, 0:63], scalar=cwT[:, cc, 0:1], in1=a3[:, :, 1:64],
                    op0=ALU.mult, op1=ALU.add), reads=[UB, AC], writes=[AC])
            for (cc, AC, a, uo, u3, a3) in halves:
                P.add("dve", lambda e, a3=a3, u3=u3, cc=cc: e.scalar_tensor_tensor(
                    out=a3[:, :, 0:63], in0=u3[:, :, 1:64], scalar=cwT[:, cc, 2:3], in1=a3[:, :, 0:63],
                    op0=ALU.mult, op1=ALU.add), reads=[UB, AC], writes=[AC])

        def gate(j):
            q3 = j % 3
            sl = 2 * q3
            qg = j % (LAG + 1)
            P.add("act", lambda e: e.activation(out=sa[q3], in_=acc[sl], func=AF.Silu), reads=["acc%d" % sl], writes=["sa%d" % q3])
            P.add("pool", lambda e: e.tensor_tensor(out=gT[qg], in0=sa[q3], in1=acc[sl + 1], op=ALU.mult),
                  reads=["sa%d" % q3, "acc%d" % (sl + 1)], writes=["gT%d" % qg])

        def down(j):
            q = j % (LAG + 1)

            def f(e):
                ins = None
                for s in range(2):
                    for h in range(2):
                        ins = e.matmul(bank(4 + s * 2 + h), lhsT=gT[q][:, s * 128:(s + 1) * 128],
                                       rhs=wdn[:, j, h * 512:(h + 1) * 512], start=(j == 0), stop=(j == 21))
                return ins
            P.add("pe", f, reads=["gT%d" % q, "wdn"], writes=["dn"])

        def finale(i):
            b, t0 = tiles[i]
            par = i % 2
            X, Y, S2 = "xt%d" % par, "yt%d" % par, "st2_%d" % par
            for s in range(2):
                P.add("act", lambda e, s=s: e.activation(out=junk, in_=bank(4 + 2 * s, 2), func=AF.Square,
                                                          accum_out=st2[par][:, s:s + 1]),
                      reads=["dn"], writes=[S2 + "a%d" % s, "junk"])
            P.add("act", lambda e: e.activation(out=st2[par][:, 2:4], in_=st2[par][:, 0:2], func=AF.Sqrt, bias=epsc[:, 0:1], scale=1.0 / D),
                  reads=[S2 + "a0", S2 + "a1"], writes=[S2 + "q"])
            P.add("dve", lambda e: e.reciprocal(out=st2[par][:, 2:4], in_=st2[par][:, 2:4]), reads=[S2 + "q"], writes=[S2 + "r"])
            for s in range(2):
                P.add("dve", lambda e, s=s: e.scalar_tensor_tensor(out=ytmp[s], in0=bank(4 + 2 * s, 2),
                                                                   scalar=st2[par][:, 2 + s:3 + s], in1=ggtF[:, b, :],
                                                                   op0=ALU.mult, op1=ALU.mult),
                      reads=["dn", S2 + "r"], writes=["ytmp%d" % s])
                P.add("pool", lambda e, s=s: e.tensor_tensor(out=xt[par][:, s, :], in0=ytmp[s], in1=xt[par][:, s, :],
                                                             op=ALU.add),
                      reads=["ytmp%d" % s, X], writes=[X, X + "s"])
            dma("pool", out_d[b, t0:t0 + 256, :].rearrange("(s p) d -> p s d", p=128), xt[par],
                reads=[X, X + "s"], slot="st" + X)

        prelude(0)
        fin_ops = []
        for i in range(len(tiles)):
            P.begin_capture()
            for j in range(LAG):
                up_pair(i, j)
                gate(j)
            head = P.end_capture()
            P.replay_merged(fin_ops, head)
            pre_ops = []
            if i + 1 < len(tiles):
                P.begin_capture()
                prelude(i + 1)
                pre_ops = P.end_capture()
            P.begin_capture()
            for j in range(LAG, 22):
                up_pair(i, j)
                gate(j)
                down(j - LAG)
            mid = P.end_capture()
            P.replay_merged(pre_ops, mid)
            for j in range(22 - LAG, 22):
                down(j)
            P.begin_capture()
            finale(i)
            fin_ops = P.end_capture()
        P.replay_merged(fin_ops, [])
        P.barrier()
        A.release(mk)

    if stage in ("mix", "mix1"):
        mixer_phase()
        P.emit()
        es.close()
        return nc
    if stage == "full":
        mixer_phase()
        ffn_phase(x1_d)
        P.emit()
        es.close()
        return nc
    if stage in ("ffn", "ffn1"):
        ffn_phase(x_d)
        P.emit()
        es.close()
        return nc

    raise NotImplementedError(stage)


def make_in_maps(inputs):
    f32 = np.float32
    x = np.asarray(inputs["x"], f32)
    c = np.asarray(inputs["c"], f32)
    ctx = np.asarray(inputs["ctx"], f32)
    c_ctx = np.asarray(inputs["c_ctx"], f32)
    hc = host_consts()
    b_ada = np.asarray(inputs["b_ada"], f32)[0]
    gs = [np.asarray(inputs[k], f32)[0] for k in ("g_mix_pre", "g_mix_post", "g_ffn_pre", "g_ffn_post")]
    conv_w = np.asarray(inputs["conv_w"], f32)[0]
    conv_b = np.asarray(inputs["conv_b"], f32)[0]
    cw = np.concatenate([conv_w, conv_b[None]], axis=0)
    shared = {
        "w_ada": np.ascontiguousarray(np.asarray(inputs["w_ada"], f32)[0]),
        "b_ada_row": np.ascontiguousarray(b_ada[None, :]),
        "b_adaT": np.ascontiguousarray(b_ada.reshape(48, 128).T),
        "gT": np.ascontiguousarray(np.stack([g.reshape(8, 128).T for g in gs], axis=1)),
        "g_row": np.ascontiguousarray(np.stack(gs, axis=0)),
        "w_in": np.ascontiguousarray(np.asarray(inputs["w_in"], f32)[0]),
        "decay": np.ascontiguousarray(np.concatenate([np.asarray(inputs["ret_decay_fwd"], f32)[0],
                                                      np.asarray(inputs["ret_decay_bwd"], f32)[0]])[None, :]),
        "w_out": np.ascontiguousarray(np.asarray(inputs["w_out"], f32)[0]),
        "w_up": np.ascontiguousarray(np.asarray(inputs["w_up"], f32)[0]),
        "cwT": np.ascontiguousarray(cw.reshape(4, 44, 128).transpose(2, 1, 0)),
        "w_down": np.ascontiguousarray(np.asarray(inputs["w_down"], f32)[0]),
    }
    shared.update(hc)
    maps = []
    for i in range(NCORES):
        m = dict(shared)
        m["x"] = np.ascontiguousarray(x[NB * i:NB * (i + 1)])
        m["ctx"] = np.ascontiguousarray(ctx[NB * i:NB * (i + 1)])
        cc = np.stack([c[NB * i], c[NB * i + 1], c_ctx], axis=0)
        m["cT"] = np.ascontiguousarray(cc.reshape(3, 8, 128).transpose(2, 1, 0))
        maps.append(m)
    return maps


_NC_CACHE = {}


def kernel(**inputs):
    maps = make_in_maps(inputs)
    if "full" not in _NC_CACHE:
        _NC_CACHE["full"] = build("full")
    nc = _NC_CACHE["full"]
    res = run_bass_kernel_spmd(nc, maps, core_ids=list(range(NCORES)))
    return np.concatenate([np.asarray(r["out"]) for r in res.results], axis=0).astype(np.float32)
```
